# Optimizing a Trainium2 kernel written in Bass

```python
import math
import jax, jax.numpy as jnp
from jax import lax
import numpy as np

D_MODEL = 1024
BATCH = 8
SEQ = 2048
DEPTH = 4

N_META = 16
N_HEADS = 16
HEAD_DIM = D_MODEL // N_HEADS
Q_BLOCK = 128
CONV_W = 3
D_FF = 4 * D_MODEL
N_MIXERS = 2
N_CONV_LAYERS = (DEPTH + 1) // 2
N_ATTN_LAYERS = DEPTH // 2
RMS_EPS = 1e-6

kernel_name = "hybrid_shortconv_stickbreaking_sqrelu"


def rmsnorm(x, g):
    xf = x.astype(jnp.float32)
    y = xf * lax.rsqrt(jnp.mean(xf * xf, axis=-1, keepdims=True) + RMS_EPS)
    return (y * g.astype(jnp.float32)).astype(x.dtype)


def causal_depthwise_conv(u, conv_w):
    L = u.shape[1]
    up = jnp.pad(u, ((0, 0), (CONV_W - 1, 0), (0, 0)))
    w = conv_w.astype(u.dtype)
    out = up[:, 0:L] * w[0]
    for k in range(1, CONV_W):
        out = out + up[:, k:k + L] * w[k]
    return out


def short_conv_mixer(h, w_in, conv_w, w_out):
    proj = jnp.einsum('bld,de->ble', h, w_in)
    gate_b = proj[..., :D_MODEL]
    gate_c = proj[..., D_MODEL:2 * D_MODEL]
    val = proj[..., 2 * D_MODEL:]
    conv = causal_depthwise_conv(gate_c * val, conv_w)
    return jnp.einsum('bld,de->ble', gate_b * conv, w_out)


def _block_bounds(L):
    bounds = [(0, N_META)]
    n_real = L - N_META
    n_blk = -(-n_real // Q_BLOCK)
    for i in range(n_blk):
        q0 = N_META + i * Q_BLOCK
        bounds.append((q0, min(q0 + Q_BLOCK, L)))
    return bounds


def stick_breaking_attention(q, k, v):
    L = q.shape[2]
    scale = 1.0 / math.sqrt(HEAD_DIM)
    outs = []
    for (q0, q1) in _block_bounds(L):
        qb = q[:, :, q0:q1]
        kb = k[:, :, :q1]
        vb = v[:, :, :q1]
        z = jnp.einsum('bhqd,bhkd->bhqk', qb, kb).astype(jnp.float32) * scale
        t_pos = q0 + jnp.arange(q1 - q0)[:, None]
        s_pos = jnp.arange(q1)[None, :]
        causal = s_pos < t_pos
        log_beta = jax.nn.log_sigmoid(z)
        log_1m = jnp.where(causal, jax.nn.log_sigmoid(-z), 0.0)
        tail = jnp.sum(log_1m, axis=-1, keepdims=True) - jnp.cumsum(log_1m, axis=-1)
        w = jnp.where(causal, jnp.exp(log_beta + tail), 0.0)
        outs.append(jnp.einsum('bhqk,bhkd->bhqd', w.astype(vb.dtype), vb))
    return jnp.concatenate(outs, axis=2)


def stick_breaking_mixer(h, w_qkv, w_out):
    B, L, _ = h.shape
    qkv = jnp.einsum('bld,de->ble', h, w_qkv)
    qkv = qkv.reshape(B, L, 3, N_HEADS, HEAD_DIM)
    q = jnp.transpose(qkv[:, :, 0], (0, 2, 1, 3))
    k = jnp.transpose(qkv[:, :, 1], (0, 2, 1, 3))
    v = jnp.transpose(qkv[:, :, 2], (0, 2, 1, 3))
    o = stick_breaking_attention(q, k, v)
    o = jnp.transpose(o, (0, 2, 1, 3)).reshape(B, L, D_MODEL)
    return jnp.einsum('bld,de->ble', o, w_out)


def sqrelu_mlp(h, w1, w2):
    a = jnp.einsum('bld,df->blf', h, w1)
    a = jnp.square(jax.nn.relu(a))
    return jnp.einsum('blf,fd->bld', a, w2)


def setup_inputs(seed: int = 0) -> dict:
    key = jax.random.key(seed)
    ks = jax.random.split(key, 16)
    f32 = jnp.float32
    D = D_MODEL

    def nrm(k, shape, scale):
        return jax.random.normal(k, shape, f32) * scale

    return {
        "x": nrm(ks[0], (BATCH, SEQ, D), 1.0),
        "meta_tokens": nrm(ks[1], (N_META, D), 1.0),
        "conv_norm": 1.0 + nrm(ks[2], (N_CONV_LAYERS, D), 0.01),
        "conv_w_in": nrm(ks[3], (N_CONV_LAYERS, D, 3 * D), D ** -0.5),
        "conv_w": nrm(ks[4], (N_CONV_LAYERS, CONV_W, D), CONV_W ** -0.5),
        "conv_w_out": nrm(ks[5], (N_CONV_LAYERS, D, D), D ** -0.5),
        "attn_norm": 1.0 + nrm(ks[6], (N_ATTN_LAYERS, D), 0.01),
        "attn_w_qkv": nrm(ks[7], (N_ATTN_LAYERS, D, 3 * D), D ** -0.5),
        "attn_w_out": nrm(ks[8], (N_ATTN_LAYERS, D, D), D ** -0.5),
        "mlp_norm": 1.0 + nrm(ks[9], (DEPTH, D), 0.01),
        "mlp_w1": nrm(ks[10], (DEPTH, D, D_FF), D ** -0.5),
        "mlp_w2": nrm(ks[11], (DEPTH, D_FF, D), D_FF ** -0.5),
        "final_norm": 1.0 + nrm(ks[12], (D,), 0.01),
    }


def reference(x, meta_tokens, conv_norm, conv_w_in, conv_w, conv_w_out,
              attn_norm, attn_w_qkv, attn_w_out, mlp_norm, mlp_w1, mlp_w2,
              final_norm):
    B = x.shape[0]
    meta = jnp.broadcast_to(meta_tokens.astype(x.dtype)[None], (B, N_META, D_MODEL))
    h = jnp.concatenate([meta, x], axis=1)
    for i in range(DEPTH):
        j = i // N_MIXERS
        if i % N_MIXERS == 0:
            h = h + short_conv_mixer(rmsnorm(h, conv_norm[j]), conv_w_in[j],
                                     conv_w[j], conv_w_out[j])
        else:
            h = h + stick_breaking_mixer(rmsnorm(h, attn_norm[j]), attn_w_qkv[j],
                                         attn_w_out[j])
        h = h + sqrelu_mlp(rmsnorm(h, mlp_norm[i]), mlp_w1[i], mlp_w2[i])
    h = rmsnorm(h, final_norm)
    return h[:, N_META:]
```

```python
from contextlib import ExitStack

import numpy as np
import ml_dtypes

import concourse.bass as bass
import concourse.mybir as mybir
from concourse.bass_utils import run_bass_kernel_spmd

F32 = mybir.dt.float32
BF16 = mybir.dt.bfloat16
AF = mybir.ActivationFunctionType
ALU = mybir.AluOpType

D = 1024
KC = 8
SEQ = 2048
NMETA = 16
T = SEQ + NMETA
DEPTH = 4
DFF = 4096
EPS = 1e-6
NEG = -30000.0
TILES = [(0, NMETA)] + [(NMETA + 512 * j, 512) for j in range(4)]
KBLK = [(128 * i, 128) for i in range(16)] + [(2048, NMETA)]
MLP_G = 4
N_WSLOT = 8
W_PREFETCH = 4


class Buf:
    __slots__ = ("w", "r")

    def __init__(self):
        self.w = None
        self.r = {}


class Plan:
    ENGS = ("pe", "act", "dve", "pool", "sp")

    def __init__(self):
        self.ops = {e: [] for e in self.ENGS}
        self.cnt = {e: 0 for e in self.ENGS}
        self.dma_cnt = {}
        self.fence_deps = {}

    def _deps(self, reads, writes, extra):
        deps = dict(self.fence_deps)

        def add(t):
            if t is None:
                return
            s, v = t
            if deps.get(s, 0) < v:
                deps[s] = v
        for t in extra:
            add(t)
        for b in reads:
            add(b.w)
        for b in writes:
            add(b.w)
            for s, v in b.r.items():
                add((s, v))
        return deps

    @staticmethod
    def _note(t, reads, writes):
        s, v = t
        for b in reads:
            if b.r.get(s, 0) < v:
                b.r[s] = v
        for b in writes:
            b.w = t
            b.r = {}

    def op(self, eng, fn, reads=(), writes=(), track=True, extra=()):
        deps = self._deps(reads, writes, extra)
        if deps.get(eng, 0) > self.cnt[eng]:
            deps[eng] = self.cnt[eng]
        if track:
            self.cnt[eng] += 1
            t = (eng, self.cnt[eng])
        else:
            t = (eng, self.cnt[eng] + 1)
        self.ops[eng].append((fn, deps, eng if track else None, 1))
        self._note(t, reads, writes)
        return t

    def dma(self, queue, fn, sem, reads=(), writes=(), extra=()):
        deps = self._deps(reads, writes, extra)
        self.dma_cnt[sem] = self.dma_cnt.get(sem, 0) + 16
        t = (sem, self.dma_cnt[sem])
        self.ops[queue].append((fn, deps, sem, 16))
        self._note(t, reads, writes)
        return t

    def fence(self):
        f = dict(self.fence_deps)
        for e in ("pe", "act", "dve", "pool"):
            f[e] = self.cnt[e]
        for s, v in self.dma_cnt.items():
            f[s] = v
        self.fence_deps = f

    def wait_all(self, eng):
        self.fence()
        self.ops[eng].append((None, dict(self.fence_deps), None, 0))

    def replay(self, eng_name, eng, sems):
        seen = {}
        for fn, deps, inc_sem, inc in self.ops[eng_name]:
            for s, v in deps.items():
                if v > 0 and seen.get(s, 0) < v:
                    eng.wait_ge(sems[s], v)
                    seen[s] = v
            if fn is None:
                continue
            ins = fn(eng)
            if inc_sem is not None:
                ins.then_inc(sems[inc_sem], inc)


def build_nc(n_layers=DEPTH, dbg=None):
    dbg = dbg or {}
    nc = bass.Bass("TRN2", target_bir_lowering=False)
    P = Plan()

    x_d = nc.dram_tensor("x", [SEQ, D], F32, kind="ExternalInput").ap()
    meta_d = nc.dram_tensor("meta", [NMETA, D], F32, kind="ExternalInput").ap()
    gains_d = nc.dram_tensor("gains", [128, 9 * KC], F32, kind="ExternalInput").ap()
    cw_d = nc.dram_tensor("cw", [128, 2 * 3 * KC], F32, kind="ExternalInput").ap()
    identf_d = nc.dram_tensor("identf", [128, 128], F32, kind="ExternalInput").ap()
    cb_d = nc.dram_tensor("cbf", [128, 6 * 128], BF16, kind="ExternalInput").ap()
    conv_w_in = nc.dram_tensor("conv_w_in", [2, D, 3 * D], F32, kind="ExternalInput").ap()
    conv_w_out = nc.dram_tensor("conv_w_out", [2, D, D], F32, kind="ExternalInput").ap()
    attn_w_qkv = nc.dram_tensor("attn_w_qkv", [2, D, 3 * D], F32, kind="ExternalInput").ap()
    attn_w_out = nc.dram_tensor("attn_w_out", [2, D, D], F32, kind="ExternalInput").ap()
    mlp_w1 = nc.dram_tensor("mlp_w1", [DEPTH, D, DFF], F32, kind="ExternalInput").ap()
    mlp_w2 = nc.dram_tensor("mlp_w2", [DEPTH, DFF, D], F32, kind="ExternalInput").ap()
    out_d = nc.dram_tensor("out", [SEQ, D], F32, kind="ExternalOutput").ap()

    uid = {"n": 0}

    def un(name):
        uid["n"] += 1
        return f"{name}_{uid['n']}"

    es = ExitStack()
    with es:
        def sb(name, shape, dt):
            return es.enter_context(nc.sbuf_tensor(name, shape, dt))

        hT = sb("hT", [128, KC, T], F32)
        hn = sb("hn", [128, KC, T], BF16)
        big2 = sb("big2", [128, KC, T], BF16)
        wring = [sb(f"wr{i}", [128, KC, 128], BF16) for i in range(N_WSLOT)]
        identf = sb("identf_s", [128, 128], F32)
        cbf = sb("cbf_s", [128, 6 * 128], BF16)
        gains = sb("gains_s", [128, 9 * KC], F32)
        cw = sb("cw_s", [128, 2 * 3 * KC], F32)
        identb = cbf[:, 0:128]
        negtri = cbf[:, 128:256]
        negones = cbf[:, 256:384]
        onesb = cbf[:, 384:512]
        m0 = cbf[:, 512:640]
        m1 = cbf[:, 640:768]

        psum = [es.enter_context(nc.psum_tensor(f"ps{i}", [128, 512], F32)) for i in range(8)]
        psum_b = [Buf() for _ in range(8)]

        NXS, NOS = 6, 4
        sem_names = ["pe", "act", "dve", "pool", "const", "metas"] + [f"xin{i}" for i in range(NXS)] + \
                    [f"outst{i}" for i in range(NOS)] + \
                    [f"w{i}" for i in range(N_WSLOT)]
        sems = {n: es.enter_context(nc.semaphore(n)) for n in sem_names}

        hT_b = [[Buf() for _ in TILES] for _ in range(KC)]
        hn_b = [[Buf() for _ in TILES] for _ in range(KC)]
        big2_b = [[Buf() for _ in TILES] for _ in range(KC)]
        wring_b = [Buf() for _ in range(N_WSLOT)]
        wstate = {"next": 0}
        rr = {"evac": 0}

        def tsl(ti):
            t0, w = TILES[ti]
            return slice(t0, t0 + w)

        P.dma("sp", lambda e: e.dma_start(out=identf[:], in_=identf_d[:, :]), "const")
        P.dma("sp", lambda e: e.dma_start(out=cbf[:], in_=cb_d[:, :]), "const")
        P.dma("sp", lambda e: e.dma_start(out=gains[:], in_=gains_d[:, :]), "const")
        P.dma("sp", lambda e: e.dma_start(out=cw[:], in_=cw_d[:, :]), "const")
        P.fence()

        wsrc = {"conv_w_in": conv_w_in, "conv_w_out": conv_w_out, "attn_w_qkv": attn_w_qkv,
                "attn_w_out": attn_w_out, "mlp_w1": mlp_w1, "mlp_w2": mlp_w2}
        wseq = []
        for l_ in range(n_layers):
            j_ = l_ // 2
            if l_ % 2 == 0:
                if not dbg.get('skip_conv'):
                    for c_ in range(KC):
                        for sec in range(3):
                            wseq.append(("conv_w_in", j_, 0, sec * D + c_ * 128, KC))
                    for cb_ in range(KC):
                        wseq.append(("conv_w_out", j_, 0, cb_ * 128, KC))
            else:
                for c_ in range(dbg.get('max_chunks', KC)):
                    for sec in range(3):
                        wseq.append(("attn_w_qkv", j_, 0, sec * D + c_ * 128, KC))
                for cb_ in range(KC):
                    wseq.append(("attn_w_out", j_, 0, cb_ * 128, KC))
            if not dbg.get('skip_mlp'):
                for g_ in range(DFF // (128 * MLP_G)):
                    for fc_ in range(MLP_G):
                        wseq.append(("mlp_w1", l_, 0, (g_ * MLP_G + fc_) * 128, KC))
                    for cb_ in range(KC):
                        wseq.append(("mlp_w2", l_, g_ * MLP_G * 128, cb_ * 128, MLP_G))
        wstate["use"] = 0
        wstate["issued"] = 0

        def issue_w(m):
            key, li, r0, c0, nk = wseq[m]
            i = m % N_WSLOT
            slot = wring[i]
            src = wsrc[key][li][r0:r0 + nk * 128, c0:c0 + 128].rearrange("(k p) c -> p k c", p=128)
            P.dma("pool", lambda e: e.dma_start(out=slot[:, 0:nk, :], in_=src), f"w{i}", writes=[wring_b[i]])

        def load_w(key, li, r0, c0, nk=KC):
            n = wstate["use"]
            assert wseq[n] == (key, li, r0, c0, nk), (n, wseq[n], (key, li, r0, c0, nk))
            wstate["use"] = n + 1
            while wstate["issued"] < min(len(wseq), n + 1 + W_PREFETCH):
                issue_w(wstate["issued"])
                wstate["issued"] += 1
            return n % N_WSLOT

        bankrr = {}

        def next_bank(banks):
            key = tuple(banks)
            i = bankrr.get(key, 0)
            bankrr[key] = i + 1
            return banks[i % len(banks)]

        def pick_evac():
            rr["evac"] ^= 1
            return "act" if rr["evac"] else "dve"

        def mm_group(bank_i, out_ap_fn, pairs, reads, first=True, last=True, track_last=True):
            n = len(pairs)
            t = None
            for idx, (l, r) in enumerate(pairs):
                st = first and idx == 0
                sp_ = last and idx == n - 1
                tr = track_last and idx == n - 1
                t = P.op("pe",
                         (lambda e, l=l, r=r, st=st, sp_=sp_: e.matmul(out_ap_fn(), lhsT=l, rhs=r, start=st, stop=sp_)),
                         reads=reads if idx == 0 else (), writes=[psum_b[bank_i]] if idx == 0 else (),
                         track=tr)
            return t

        def proj(wkey, li, r0, c0, in_buf, in_b, banks, evac, tiles=range(len(TILES)), fine=False):
            wi = load_w(wkey, li, r0, c0)
            for ti in tiles:
                bi = next_bank(banks)
                t0, w = TILES[ti]
                rd = [wring_b[wi]] + [in_b[k][ti] for k in range(KC)]
                for k in range(KC):
                    edge = (k == 0 or k == KC - 1)
                    P.op("pe", (lambda e, bi=bi, w=w, k=k, t0=t0: e.matmul(
                        psum[bi][:, 0:w], lhsT=wring[wi][:, k, :], rhs=in_buf[:, k, t0:t0 + w],
                        start=(k == 0), stop=(k == KC - 1))),
                        reads=rd if edge else (), writes=[psum_b[bi]] if edge else (), track=(k == KC - 1))
                    if fine and k % 2 == 1 and k < KC - 1:
                        yield
                evac(ti, bi)
                yield

        def run(gen):
            for _ in gen:
                pass

        def phase_input(outer=None):
            with ExitStack() as own:
                ph = own if outer is None else outer
                nxs = NXS if outer is None else 3
                xs = [ph.enter_context(nc.sbuf_tensor(un(f"xs{i}"), [128, D], F32)) for i in range(nxs)]
                ms = ph.enter_context(nc.sbuf_tensor(un("ms"), [NMETA, D], F32))
                xs_b = [Buf() for _ in range(NXS)]
                ms_b = Buf()
                P.dma("sp", lambda e: e.dma_start(out=ms[:], in_=meta_d[:, :]), "metas", writes=[ms_b])
                bi = 0
                for k in range(KC):
                    P.op("pe", (lambda e, k=k: e.transpose(psum[0][:, k * NMETA:(k + 1) * NMETA],
                                                           ms[0:NMETA, k * 128:(k + 1) * 128],
                                                           identf[0:NMETA, 0:NMETA])),
                         reads=[ms_b] if k == 0 else (), writes=[psum_b[0]] if k == 0 else (), track=(k == KC - 1))
                P.op("dve", lambda e: e.tensor_copy(out=hT[:, :, 0:NMETA],
                                                    in_=psum[0][:, 0:KC * NMETA].rearrange("p (k t) -> p k t", t=NMETA)),
                     reads=[psum_b[0]], writes=[hT_b[k][0] for k in range(KC)])
                bank_rr = 1
                for b in range(16):
                    s = b % nxs
                    P.dma("sp", (lambda e, b=b, s=s: e.dma_start(out=xs[s][:], in_=x_d[b * 128:(b + 1) * 128, :])),
                          f"xin{s}", writes=[xs_b[s]])
                    ti = 1 + b // 4
                    col = NMETA + b * 128
                    for half in range(2):
                        bi = 1 + (bank_rr % 7)
                        bank_rr += 1
                        for q in range(4):
                            k = half * 4 + q
                            P.op("pe", (lambda e, k=k, q=q, s=s, bi=bi: e.transpose(
                                psum[bi][:, q * 128:(q + 1) * 128], xs[s][:, k * 128:(k + 1) * 128], identf[:, :])),
                                reads=[xs_b[s]] if q == 0 else (), writes=[psum_b[bi]] if q == 0 else (),
                                track=(q == 3))
                        eng = pick_evac()
                        src = (lambda bi=bi: psum[bi][:, 0:512].rearrange("p (k t) -> p k t", t=128))
                        dst = (lambda half=half, col=col: hT[:, half * 4:half * 4 + 4, col:col + 128])
                        if eng == "act":
                            fn = (lambda e, src=src, dst=dst: e.activation(out=dst(), in_=src(), func=AF.Copy))
                        else:
                            fn = (lambda e, src=src, dst=dst: e.tensor_copy(out=dst(), in_=src()))
                        P.op(eng, fn, reads=[psum_b[bi]], writes=[hT_b[half * 4 + q][ti] for q in range(4)])
                if outer is None:
                    P.fence()

        def phase_norm(gidx, final=False, outer=None, alias_big2=False):
            own = ExitStack() if (outer is None and not alias_big2) else None
            ph = own if own is not None else outer
            if alias_big2:
                hsq_ap = [(lambda w: big2[:, :, 0:w]), (lambda w: big2[:, :, 512:512 + w])]
                hsq_tr = [[big2_b[k][ti] for k in range(KC) for ti in (0, 1)],
                          [big2_b[k][ti] for k in range(KC) for ti in (1, 2)]]
            else:
                hsq = [ph.enter_context(nc.sbuf_tensor(un(f"hsq{i}"), [128, KC, 512], BF16)) for i in range(2)]
                hsq_ap = [(lambda w, i=i: hsq[i][:, :, 0:w]) for i in range(2)]
                hsq_tr = [[Buf()], [Buf()]]
            for ti, (t0, w) in enumerate(TILES):
                s = ti % 2
                bi = 6 + s
                P.op("act", (lambda e, s=s, t0=t0, w=w: e.activation(
                    out=hsq_ap[s](w), in_=hT[:, :, t0:t0 + w], func=AF.Square)),
                    reads=[hT_b[k][ti] for k in range(KC)], writes=hsq_tr[s])
                for k in range(KC):
                    P.op("pe", (lambda e, s=s, bi=bi, w=w, k=k: e.matmul(
                        psum[bi][:, 0:w], lhsT=onesb, rhs=hsq_ap[s](w)[:, k, :], start=(k == 0), stop=(k == KC - 1))),
                        reads=hsq_tr[s] if k in (0, KC - 1) else (), writes=[psum_b[bi]] if k in (0, KC - 1) else (),
                        track=(k == KC - 1))
                P.op("act", (lambda e, bi=bi, w=w: e.activation(
                    out=psum[bi][:, 0:w], in_=psum[bi][:, 0:w], func=AF.Ln, scale=1.0 / D, bias=EPS)),
                    reads=[psum_b[bi]], writes=[psum_b[bi]])
                P.op("act", (lambda e, bi=bi, w=w: e.activation(
                    out=psum[bi][:, 0:w], in_=psum[bi][:, 0:w], func=AF.Exp, scale=-0.5)),
                    reads=[psum_b[bi]], writes=[psum_b[bi]])
                for k in range(KC):
                    g_ap = gains[:, gidx * KC + k:gidx * KC + k + 1]
                    if final:
                        P.op("dve", (lambda e, k=k, bi=bi, t0=t0, w=w, g_ap=g_ap: e.scalar_tensor_tensor(
                            out=hT[:, k, t0:t0 + w], in0=hT[:, k, t0:t0 + w], scalar=g_ap,
                            in1=psum[bi][:, 0:w], op0=ALU.mult, op1=ALU.mult)),
                            reads=[psum_b[bi]], writes=[hT_b[k][ti]])
                    else:
                        P.op("dve", (lambda e, k=k, bi=bi, t0=t0, w=w, g_ap=g_ap: e.scalar_tensor_tensor(
                            out=hn[:, k, t0:t0 + w], in0=hT[:, k, t0:t0 + w], scalar=g_ap,
                            in1=psum[bi][:, 0:w], op0=ALU.mult, op1=ALU.mult)),
                            reads=[psum_b[bi], hT_b[k][ti]], writes=[hn_b[k][ti]])
            if own is not None:
                P.fence()
                own.close()

        def resid_evac(cb):
            def ev(ti, bi):
                t0, w = TILES[ti]
                P.op("dve", (lambda e: e.tensor_tensor(out=hT[:, cb, t0:t0 + w], in0=psum[bi][:, 0:w],
                                                       in1=hT[:, cb, t0:t0 + w], op=ALU.add)),
                     reads=[psum_b[bi]], writes=[hT_b[cb][ti]])
            return ev

        def phase_conv(j, with_input=False):
            with ExitStack() as ph:
                vsb = [ph.enter_context(nc.sbuf_tensor(un(f"vsb{i}"), [128, 512], F32)) for i in range(2)]
                t1 = [ph.enter_context(nc.sbuf_tensor(un(f"t1{i}"), [128, 512], F32)) for i in range(2)]
                ub = [ph.enter_context(nc.sbuf_tensor(un(f"u{i}"), [128, T + 2], F32)) for i in range(2)]
                vsb_b = [Buf(), Buf()]
                t1_b = [Buf(), Buf()]
                u_b = [[Buf() for _ in TILES] for _ in range(2)]
                uh_b = [Buf(), Buf()]
                for i in range(2):
                    P.op("pool", (lambda e, i=i: e.memset(ub[i][:, 0:2], 0.0)), writes=[uh_b[i]])
                if with_input:
                    phase_input(outer=ph)
                phase_norm(0 + j, outer=ph)
                cnt = {"n": 0}
                for c in range(KC):
                    us = c % 2
                    held = {}

                    def ev_gb(ti, bi):
                        held[("gb", ti)] = bi

                    def ev_gc(ti, bi):
                        held[("gc", ti)] = bi

                    def ev_val(ti, place, c=c, us=us):
                        t0, w = TILES[ti]
                        s = cnt["n"] % 2
                        cnt["n"] += 1
                        bi, co = place
                        bgb, cgb = held[("gb", ti)]
                        bgc, cgc = held[("gc", ti)]
                        P.op("act", (lambda e: e.activation(out=vsb[s][:, 0:w], in_=psum[bi][:, co:co + w], func=AF.Copy)),
                             reads=[psum_b[bi]], writes=[vsb_b[s]])
                        P.op("dve", (lambda e: e.tensor_tensor(out=ub[us][:, 2 + t0:2 + t0 + w], in0=psum[bgc][:, cgc:cgc + w],
                                                               in1=vsb[s][:, 0:w], op=ALU.mult)),
                             reads=[psum_b[bgc], vsb_b[s]], writes=[u_b[us][ti]])
                        prev = [u_b[us][ti - 1]] if ti > 0 else [uh_b[us]]
                        base = (j * 3) * KC + c
                        w0 = cw[:, base:base + 1]
                        w1 = cw[:, base + KC:base + KC + 1]
                        w2 = cw[:, base + 2 * KC:base + 2 * KC + 1]
                        P.op("dve", (lambda e: e.tensor_scalar(out=t1[s][:, 0:w], in0=ub[us][:, t0:t0 + w],
                                                               scalar1=w0, scalar2=None, op0=ALU.mult)),
                             reads=[u_b[us][ti]] + prev, writes=[t1_b[s]])
                        P.op("dve", (lambda e: e.scalar_tensor_tensor(out=t1[s][:, 0:w], in0=ub[us][:, 1 + t0:1 + t0 + w],
                                                                      scalar=w1, in1=t1[s][:, 0:w],
                                                                      op0=ALU.mult, op1=ALU.add)),
                             reads=[u_b[us][ti]] + prev, writes=[t1_b[s]])
                        P.op("dve", (lambda e: e.scalar_tensor_tensor(out=t1[s][:, 0:w], in0=ub[us][:, 2 + t0:2 + t0 + w],
                                                                      scalar=w2, in1=t1[s][:, 0:w],
                                                                      op0=ALU.mult, op1=ALU.add)),
                             reads=[u_b[us][ti]], writes=[t1_b[s]])
                        P.op("dve", (lambda e: e.tensor_tensor(out=big2[:, c, t0:t0 + w], in0=psum[bgb][:, cgb:cgb + w],
                                                               in1=t1[s][:, 0:w], op=ALU.mult)),
                             reads=[psum_b[bgb], t1_b[s]], writes=[big2_b[c][ti]])

                    wi_gb = load_w("conv_w_in", j, 0, c * 128)
                    wi_gc = load_w("conv_w_in", j, 0, D + c * 128)
                    wi_v = load_w("conv_w_in", j, 0, 2 * D + c * 128)
                    for ti, (t0, w) in enumerate(TILES):
                        if ti == 0:
                            b0 = next_bank([6, 7])
                            places = [(b0, 0), (b0, 16), (b0, 32)]
                        else:
                            b0 = next_bank([0, 3])
                            places = [(b0, 0), (b0 + 1, 0), (b0 + 2, 0)]
                        for n_, (wi, ev) in enumerate(((wi_gb, ev_gb), (wi_gc, ev_gc), (wi_v, ev_val))):
                            bi, co = places[n_]
                            pairs = [(wring[wi][:, k, :], hn[:, k, t0:t0 + w]) for k in range(KC)]
                            mm_group(bi, (lambda bi=bi, w=w, co=co: psum[bi][:, co:co + w]), pairs,
                                     reads=[wring_b[wi]] + [hn_b[k][ti] for k in range(KC)])
                            ev(ti, (bi, co))
                for cb in range(KC):
                    run(proj("conv_w_out", j, 0, cb * 128, big2, big2_b, [0, 1, 2, 3, 4, 5], resid_evac(cb)))
                P.fence()

        def phase_mlp(l):
            with ExitStack() as ph:
                ab = [ph.enter_context(nc.sbuf_tensor(un(f"ab{i}"), [128, MLP_G, T], BF16)) for i in range(2)]
                rt = [ph.enter_context(nc.sbuf_tensor(un(f"rt{i}"), [128, 512], F32)) for i in range(2)]
                ab_b = [[[Buf() for _ in TILES] for _ in range(MLP_G)] for _ in range(2)]
                rt_b = [Buf(), Buf()]
                cnt = {"n": 0}
                phase_norm(4 + l, outer=ph)
                ngroups = DFF // (128 * MLP_G)
                for g in range(ngroups):
                    a = g % 2
                    wis = [load_w("mlp_w1", l, 0, (g * MLP_G + fc) * 128) for fc in range(MLP_G)]
                    for ti, (t0, w) in enumerate(TILES):
                        for fc in range(MLP_G):
                            wi = wis[fc]
                            bi = next_bank([0, 1, 2, 3])
                            pairs = [(wring[wi][:, k, :], hn[:, k, t0:t0 + w]) for k in range(KC)]
                            mm_group(bi, (lambda bi=bi, w=w: psum[bi][:, 0:w]), pairs,
                                     reads=[wring_b[wi]] + [hn_b[k][ti] for k in range(KC)])
                            s = cnt["n"] % 2
                            cnt["n"] += 1
                            P.op("act", (lambda e, s=s, bi=bi, w=w: e.activation(out=rt[s][:, 0:w], in_=psum[bi][:, 0:w],
                                                                                func=AF.Relu)),
                                 reads=[psum_b[bi]], writes=[rt_b[s]])
                            P.op("pool", (lambda e, s=s, a=a, fc=fc, t0=t0, w=w: e.tensor_tensor(
                                out=ab[a][:, fc, t0:t0 + w], in0=rt[s][:, 0:w], in1=rt[s][:, 0:w], op=ALU.mult)),
                                reads=[rt_b[s]], writes=[ab_b[a][fc][ti]])
                    for cb in range(KC):
                        i = load_w("mlp_w2", l, g * MLP_G * 128, cb * 128, MLP_G)
                        slot = wring[i]
                        ev2 = resid_evac(cb)
                        for ti, (t0, w) in enumerate(TILES):
                            bi = next_bank([4, 5, 6, 7])
                            pairs = [(slot[:, k, :], ab[a][:, k, t0:t0 + w]) for k in range(MLP_G)]
                            mm_group(bi, (lambda bi=bi, w=w: psum[bi][:, 0:w]), pairs,
                                     reads=[wring_b[i]] + [ab_b[a][k][ti] for k in range(MLP_G)])
                            ev2(ti, bi)
                P.fence()

        def phase_attn(j):
            with ExitStack() as ph:
                def psb(name, shape, dt):
                    return ph.enter_context(nc.sbuf_tensor(un(name), shape, dt))
                qA = [psb(f"qA{i}", [128, T], BF16) for i in range(2)]
                qB = [psb(f"qB{i}", [128, T], BF16) for i in range(2)]
                kT = [psb(f"kT{i}", [128, T], BF16) for i in range(2)]
                vA = [psb(f"vA{i}", [128, 17, 128], BF16) for i in range(2)]
                vB = [psb(f"vB{i}", [128, 17, 128], BF16) for i in range(2)]
                NS = 3
                ebuf = [psb(f"eb{i}", [128, 512], F32) for i in range(2)]
                spb = [psb(f"spb{i}", [128, 512], BF16) for i in range(NS)]
                wb = [psb(f"wb{i}", [128, 512], BF16) for i in range(NS)]
                spacc_h = [psb(f"spacc{h}", [128, 512], F32) for h in range(2)]
                spaccb_h = [[psb(f"spaccb{h}_{i}", [128, 512], BF16) for i in range(2)] for h in range(2)]
                q_b = [[Buf() for _ in TILES] for _ in range(2)]
                k_b = [[Buf() for _ in TILES] for _ in range(2)]
                v_b = [[Buf() for _ in range(5)] for _ in range(2)]
                ebuf_b = [Buf(), Buf()]
                spb_b = [Buf() for _ in range(NS)]
                wb_b = [Buf() for _ in range(NS)]
                spacc_bh = [Buf(), Buf()]
                spaccb_bh = [[Buf(), Buf()], [Buf(), Buf()]]
                for i in range(2):
                    zb_ = Buf()
                    eng = "dve" if i == 0 else "pool"
                    P.op(eng, (lambda e, i=i: e.memset(qA[i][64:128, :], 0.0)), writes=[zb_])
                    P.op(eng, (lambda e, i=i: e.memset(qB[i][0:64, :], 0.0)), writes=[zb_])
                    P.op(eng, (lambda e, i=i: e.memset(vA[i][:, :, 64:128], 0.0)), writes=[zb_])
                    P.op(eng, (lambda e, i=i: e.memset(vB[i][:, :, 0:64], 0.0)), writes=[zb_])
                    for ti in range(len(TILES)):
                        q_b[i][ti].w = zb_.w
                    for gi in range(5):
                        v_b[i][gi].w = zb_.w
                phase_norm(2 + j, alias_big2=True)
                PROJ_BANKS = [0, 1] if False else [6, 7]

                def tiles_of(ks, nk):
                    return [ti for ti, (t0, w) in enumerate(TILES) if t0 < ks + nk and ks < t0 + w]

                def qkv_proj(c):
                    s = c % 2

                    def ev_q(ti, bi):
                        t0, w = TILES[ti]
                        P.op("dve", (lambda e: e.tensor_scalar(out=qA[s][0:64, t0:t0 + w], in0=psum[bi][0:64, 0:w],
                                                               scalar1=0.125, scalar2=None, op0=ALU.mult)),
                             reads=[psum_b[bi]], writes=[q_b[s][ti]])
                        P.op("dve", (lambda e: e.tensor_scalar(out=qB[s][64:128, t0:t0 + w], in0=psum[bi][64:128, 0:w],
                                                               scalar1=0.125, scalar2=None, op0=ALU.mult)),
                             reads=[psum_b[bi]], writes=[q_b[s][ti]])

                    def ev_k(ti, bi):
                        t0, w = TILES[ti]
                        P.op("dve", (lambda e: e.tensor_copy(out=kT[s][:, t0:t0 + w], in_=psum[bi][:, 0:w])),
                             reads=[psum_b[bi]], writes=[k_b[s][ti]])
                    yield from proj("attn_w_qkv", j, 0, c * 128, hn, hn_b, PROJ_BANKS, ev_q, fine=True)
                    yield from proj("attn_w_qkv", j, 0, D + c * 128, hn, hn_b, PROJ_BANKS, ev_k, fine=True)
                    wi = load_w("attn_w_qkv", j, 0, 2 * D + c * 128)
                    groups = [[4 * g + q for q in range(4)] for g in range(4)] + [[16]]
                    for gi, blks in enumerate(groups):
                        bi = next_bank(PROJ_BANKS)
                        for q, kb in enumerate(blks):
                            ks, nk = KBLK[kb]
                            tis = tiles_of(ks, nk)
                            rd = [wring_b[wi]] + [hn_b[kk][ti] for kk in range(KC) for ti in tis]
                            for k in range(KC):
                                edge = (k == 0 or k == KC - 1)
                                P.op("pe", (lambda e, bi=bi, q=q, nk=nk, ks=ks, k=k: e.matmul(
                                    psum[bi][0:nk, q * 128:(q + 1) * 128], lhsT=hn[:, k, ks:ks + nk],
                                    rhs=wring[wi][:, k, :], start=(k == 0), stop=(k == KC - 1))),
                                    reads=rd if edge else (), writes=[psum_b[bi]] if edge else (),
                                    track=(k == KC - 1))
                                if k == 3:
                                    yield
                            yield
                        nb = len(blks)
                        nk = KBLK[blks[0]][1]
                        kb0 = blks[0]
                        src = (lambda bi=bi, nb=nb, nk=nk: psum[bi][0:nk, 0:nb * 128].rearrange("p (b c) -> p b c", c=128))
                        P.op("dve", (lambda e, src=src, nk=nk, kb0=kb0, nb=nb: e.tensor_copy(
                            out=vA[s][0:nk, kb0:kb0 + nb, 0:64], in_=src()[:, :, 0:64])),
                            reads=[psum_b[bi]], writes=[v_b[s][gi]])
                        P.op("dve", (lambda e, src=src, nk=nk, kb0=kb0, nb=nb: e.tensor_copy(
                            out=vB[s][0:nk, kb0:kb0 + nb, 64:128], in_=src()[:, :, 64:128])),
                            reads=[psum_b[bi]], writes=[v_b[s][gi]])
                        yield

                def pairs_for_chunk(c):
                    lst = []
                    blks_of = {}
                    for tj in range(dbg.get('max_tj', len(TILES))):
                        t0, W_ = TILES[tj]
                        if tj == 0:
                            blks = [dict(i=0, nk=NMETA, c0=0, N=NMETA, mask=m0)]
                        else:
                            blks = []
                            for i in range(4 * tj, -1, -1):
                                r = i - 4 * (tj - 1)
                                if r == 4:
                                    blks.append(dict(i=i, nk=NMETA, c0=496, N=16, mask=m0))
                                elif r >= 1:
                                    blks.append(dict(i=i, nk=128, c0=128 * r - 16, N=528 - 128 * r, mask=m0))
                                elif r == 0:
                                    blks.append(dict(i=i, nk=128, c0=0, N=512, mask=m1))
                                else:
                                    blks.append(dict(i=i, nk=128, c0=0, N=512, mask=None))
                        blks_of[tj] = blks
                    ntl = len(blks_of)
                    streams = []
                    for head, order in ((0, list(range(ntl))), (1, list(range(ntl - 1, -1, -1)))):
                        st = []
                        for tj in order:
                            blks = blks_of[tj]
                            for bn, b in enumerate(blks):
                                d = dict(b)
                                d.update(c=c, tj=tj, head=head, ks=KBLK[b["i"]][0],
                                         first=(bn == 0), last=(bn == len(blks) - 1))
                                d["gfirst"] = d["first"]
                                d["glast"] = d["last"]
                                d["ktis"] = tiles_of(d["ks"], d["nk"])
                                d["vg"] = b["i"] // 4
                                d["nc0"] = blks[bn + 1]["c0"] if bn + 1 < len(blks) else None
                                st.append(d)
                        streams.append(st)
                    for a_, b_ in zip(*streams):
                        lst.append(a_)
                        lst.append(b_)
                    return lst

                state = {"n": 0, "ob": 0}
                ZE = [2, 3, 4, 5]

                def stage_A(p):
                    s = p["c"] % 2
                    n = p["n"]
                    zb = ZE[n % 4]
                    ks, nk = p["ks"], p["nk"]
                    t0, _ = TILES[p["tj"]]
                    q0 = t0 + p["c0"]
                    N = p["N"]
                    qh = qA[s] if p["head"] == 0 else qB[s]
                    msk = p["mask"]
                    P.op("pe", (lambda e: e.matmul(psum[zb][0:nk, 0:N], lhsT=kT[s][:, ks:ks + nk], rhs=qh[:, q0:q0 + N],
                                                   start=True, stop=False, skip_group_check=True)),
                         reads=[k_b[s][ti] for ti in p["ktis"]] + [q_b[s][p["tj"]]], writes=[psum_b[zb]],
                         track=(msk is None))
                    if msk is not None:
                        nm = min(N, 128)
                        P.op("pe", (lambda e: e.matmul(psum[zb][0:nk, 0:nm], lhsT=identb[:, 0:nk], rhs=msk[:, 0:nm],
                                                       start=False, stop=False, skip_group_check=True)))
                    es_ = n % 2
                    P.op("act", (lambda e: e.activation(out=ebuf[es_][0:nk, 0:N], in_=psum[zb][0:nk, 0:N], func=AF.Exp)),
                         reads=[psum_b[zb]], writes=[ebuf_b[es_]])

                def stage_A2(p):
                    n = p["n"]
                    nk = p["nk"]
                    N = p["N"]
                    es_ = n % 2
                    ss_ = n % NS
                    P.op("act", (lambda e: e.activation(out=spb[ss_][0:nk, 0:N], in_=ebuf[es_][0:nk, 0:N], func=AF.Ln,
                                                        bias=1.0, scale=1.0)),
                         reads=[ebuf_b[es_]], writes=[spb_b[ss_]])

                def stage_B(p):
                    n = p["n"]
                    zb = ZE[n % 4]
                    nk = p["nk"]
                    t0, W_ = TILES[p["tj"]]
                    c0 = p["c0"]
                    N = p["N"]
                    ss_ = n % NS
                    P.op("pe", (lambda e: e.matmul(psum[zb][0:nk, 0:N], lhsT=negtri[0:nk, 0:nk], rhs=spb[ss_][0:nk, 0:N],
                                                   start=False, stop=p["first"], skip_group_check=True)),
                         reads=[spb_b[ss_]], writes=[psum_b[zb]], track=p["first"])
                    hd = p["head"]
                    spacc = spacc_h[hd]
                    spaccb = spaccb_h[hd]
                    spacc_b = spacc_bh[hd]
                    spaccb_b = spaccb_bh[hd]
                    if not p["first"]:
                        sa = p["sa"]
                        P.op("pe", (lambda e: e.matmul(psum[zb][0:nk, 0:N], lhsT=negones[:, 0:nk],
                                                       rhs=spaccb[sa][:, c0:c0 + N], start=False, stop=True,
                                                       skip_group_check=True)),
                             reads=[spaccb_b[sa]], writes=[psum_b[zb]])
                    if p["first"] and not p["last"]:
                        P.op("pool", (lambda e: e.memset(spacc[:, 0:W_], 0.0)), writes=[spacc_b])
                    if not p["last"]:
                        P.op("dve", (lambda e: e.tensor_tensor(out=spacc[0:nk, c0:W_], in0=spacc[0:nk, c0:W_],
                                                               in1=spb[ss_][0:nk, 0:N], op=ALU.add)),
                             reads=[spb_b[ss_]], writes=[spacc_b])
                        nsa = p["nsa"]
                        nc0 = p["nc0"]
                        P.op("dve", (lambda e: e.tensor_copy(out=spaccb[nsa][:, nc0:W_], in_=spacc[:, nc0:W_])),
                             reads=[spacc_b], writes=[spaccb_b[nsa]])
                    P.op("act", (lambda e: e.activation(out=wb[ss_][0:nk, 0:N], in_=psum[zb][0:nk, 0:N], func=AF.Exp)),
                         reads=[psum_b[zb]], writes=[wb_b[ss_]])

                def stage_C(p):
                    s = p["c"] % 2
                    n = p["n"]
                    nk = p["nk"]
                    t0, W_ = TILES[p["tj"]]
                    c0 = p["c0"]
                    N = p["N"]
                    ss_ = n % NS
                    ob = p["ob"]
                    vh = vA[s] if p["head"] == 0 else vB[s]
                    P.op("pe", (lambda e: e.matmul(psum[ob][:, c0:c0 + N], lhsT=vh[0:nk, p["i"], :], rhs=wb[ss_][0:nk, 0:N],
                                                   start=p["gfirst"], stop=p["glast"], skip_group_check=True)),
                         reads=[wb_b[ss_], v_b[s][p["vg"]]], writes=[psum_b[ob]] if (p["gfirst"] or p["glast"]) else (),
                         track=p["glast"])
                    if p["glast"]:
                        c = p["c"]
                        tj = p["tj"]
                        h0 = 64 * p["head"]
                        P.op("dve", (lambda e: e.tensor_copy(out=big2[h0:h0 + 64, c, t0:t0 + W_],
                                                             in_=psum[ob][h0:h0 + 64, 0:W_])),
                             reads=[psum_b[ob]], writes=[big2_b[c][tj]])

                run(qkv_proj(0))
                NCH = dbg.get('max_chunks', KC)
                for c in range(NCH):
                    nxt = qkv_proj(c + 1) if c + 1 < NCH else iter(())
                    lst = pairs_for_chunk(c)
                    for p in lst:
                        p["n"] = state["n"]
                        state["n"] += 1
                    sa_h = [0, 0]
                    for idx, p in enumerate(lst):
                        p["ob"] = p["head"]
                        hd = p["head"]
                        p["sa"] = sa_h[hd]
                        if not p["last"]:
                            sa_h[hd] ^= 1
                            p["nsa"] = sa_h[hd]
                    npairs = len(lst)
                    for step in range(npairs + 3):
                        if step < npairs:
                            stage_A(lst[step])
                        if 1 <= step <= npairs:
                            stage_A2(lst[step - 1])
                        if 2 <= step <= npairs + 1:
                            stage_B(lst[step - 2])
                        if step >= 3:
                            stage_C(lst[step - 3])
                        next(nxt, None)
                    run(nxt)
                for cb in range(KC):
                    run(proj("attn_w_out", j, 0, cb * 128, big2, big2_b, [0, 1, 2, 3, 4, 5], resid_evac(cb)))
                P.fence()

        def phase_output():
            with ExitStack() as ph:
                phase_norm(8, final=True, outer=ph)
                ost = [ph.enter_context(nc.sbuf_tensor(un(f"ost{i}"), [128, D], F32)) for i in range(NOS)]
                ost_b = [Buf() for _ in range(NOS)]
                bank_rr = 0
                for b in range(16):
                    s = b % NOS
                    ti = 1 + b // 4
                    col = NMETA + b * 128
                    for half in range(2):
                        bi = bank_rr % 8
                        bank_rr += 1
                        for q in range(4):
                            k = half * 4 + q
                            P.op("pe", (lambda e, k=k, q=q, bi=bi, col=col: e.transpose(
                                psum[bi][:, q * 128:(q + 1) * 128], hT[:, k, col:col + 128], identf[:, :])),
                                reads=[hT_b[kk][ti] for kk in range(half * 4, half * 4 + 4)] if q == 0 else (),
                                writes=[psum_b[bi]] if q == 0 else (), track=(q == 3))
                        eng = pick_evac()
                        if eng == "act":
                            fn = (lambda e, s=s, half=half, bi=bi: e.activation(
                                out=ost[s][:, half * 512:(half + 1) * 512], in_=psum[bi][:, 0:512], func=AF.Copy))
                        else:
                            fn = (lambda e, s=s, half=half, bi=bi: e.tensor_copy(
                                out=ost[s][:, half * 512:(half + 1) * 512], in_=psum[bi][:, 0:512]))
                        P.op(eng, fn, reads=[psum_b[bi]], writes=[ost_b[s]])
                    P.dma("sp", (lambda e, b=b, s=s: e.dma_start(out=out_d[b * 128:(b + 1) * 128, :], in_=ost[s][:])),
                          f"outst{s}", reads=[ost_b[s]])
                P.wait_all("sp")

        fuse_in = n_layers > 0 and not dbg.get('skip_conv')
        if not fuse_in:
            phase_input()
        for l in range(n_layers):
            j = l // 2
            if l % 2 == 0:
                if not dbg.get('skip_conv'):
                    phase_conv(j, with_input=(l == 0))
            else:
                phase_attn(j)
            if not dbg.get('skip_mlp'):
                phase_mlp(l)
        phase_output()

        with nc.Block() as block:
            @block.tensor
            def _(e):
                P.replay("pe", e, sems)

            @block.scalar
            def _(e):
                P.replay("act", e, sems)

            @block.vector
            def _(e):
                P.replay("dve", e, sems)

            @block.gpsimd
            def _(e):
                P.replay("pool", e, sems)

            @block.sync
            def _(e):
                P.replay("sp", e, sems)
    return nc, P


def _consts():
    bf = ml_dtypes.bfloat16
    ident = np.eye(128, dtype=np.float32)
    j = np.arange(128)[:, None]
    s = np.arange(128)[None, :]
    negtri = np.where(j >= s, -1.0, 0.0).astype(np.float32)
    negones = -np.ones((128, 128), np.float32)
    ones = np.ones((128, 128), np.float32)
    m0 = np.where(s <= j, NEG, 0.0).astype(np.float32)
    m1 = np.where(s <= j - 16, NEG, 0.0).astype(np.float32)
    cb = np.concatenate([ident, negtri, negones, ones, m0, m1], axis=1).astype(bf)
    return ident, cb


def _layout_vec(v):
    v = np.asarray(v, np.float32).reshape(-1, KC, 128)
    return np.ascontiguousarray(v.transpose(2, 0, 1).reshape(128, -1))


_CACHE = {}


def kernel(x, meta_tokens, conv_norm, conv_w_in, conv_w, conv_w_out, attn_norm, attn_w_qkv, attn_w_out,
           mlp_norm, mlp_w1, mlp_w2, final_norm, _n_layers=DEPTH, _dbg=None):
    x = np.asarray(x, np.float32)
    B = x.shape[0]
    key = (_n_layers, repr(_dbg))
    if key not in _CACHE:
        _CACHE[key] = build_nc(_n_layers, _dbg)[0]
    nc = _CACHE[key]
    ident, cb = _consts()
    gains = _layout_vec(np.concatenate([np.asarray(conv_norm), np.asarray(attn_norm), np.asarray(mlp_norm),
                                        np.asarray(final_norm)[None]], axis=0))
    cw = _layout_vec(np.asarray(conv_w).reshape(6, D))
    common = {
        "meta": np.ascontiguousarray(np.asarray(meta_tokens, np.float32)),
        "gains": gains, "cw": cw, "identf": ident, "cbf": cb,
        "conv_w_in": np.ascontiguousarray(np.asarray(conv_w_in, np.float32)),
        "conv_w_out": np.ascontiguousarray(np.asarray(conv_w_out, np.float32)),
        "attn_w_qkv": np.ascontiguousarray(np.asarray(attn_w_qkv, np.float32)),
        "attn_w_out": np.ascontiguousarray(np.asarray(attn_w_out, np.float32)),
        "mlp_w1": np.ascontiguousarray(np.asarray(mlp_w1, np.float32)),
        "mlp_w2": np.ascontiguousarray(np.asarray(mlp_w2, np.float32)),
    }
    in_maps = [dict(common, x=np.ascontiguousarray(x[b])) for b in range(B)]
    res = run_bass_kernel_spmd(nc, in_maps, core_ids=list(range(B)))
    return np.stack([np.asarray(r["out"], np.float32) for r in res.results], axis=0)
```

```python
from contextlib import ExitStack

import numpy as np
import ml_dtypes

import concourse.bass as bass
import concourse.mybir as mybir
from concourse.bass_utils import run_bass_kernel_spmd

F32 = mybir.dt.float32
BF16 = mybir.dt.bfloat16
AF = mybir.ActivationFunctionType
ALU = mybir.AluOpType

D = 1024
KC = 8
SEQ = 2048
NMETA = 16
T = SEQ + NMETA
DEPTH = 4
DFF = 4096
EPS = 1e-6
NEG = -30000.0
TILES = [(0, NMETA)] + [(NMETA + 512 * j, 512) for j in range(4)]
KBLK = [(128 * i, 128) for i in range(16)] + [(2048, NMETA)]
MLP_G = 4
N_WSLOT = 8
W_PREFETCH = 4


class Buf:
    __slots__ = ("w", "r")

    def __init__(self):
        self.w = None
        self.r = {}


class Plan:
    ENGS = ("pe", "act", "dve", "pool", "sp")

    def __init__(self):
        self.ops = {e: [] for e in self.ENGS}
        self.cnt = {e: 0 for e in self.ENGS}
        self.dma_cnt = {}
        self.fence_deps = {}

    def _deps(self, reads, writes, extra):
        deps = dict(self.fence_deps)

        def add(t):
            if t is None:
                return
            s, v = t
            if deps.get(s, 0) < v:
                deps[s] = v
        for t in extra:
            add(t)
        for b in reads:
            add(b.w)
        for b in writes:
            add(b.w)
            for s, v in b.r.items():
                add((s, v))
        return deps

    @staticmethod
    def _note(t, reads, writes):
        s, v = t
        for b in reads:
            if b.r.get(s, 0) < v:
                b.r[s] = v
        for b in writes:
            b.w = t
            b.r = {}

    def op(self, eng, fn, reads=(), writes=(), track=True, extra=()):
        deps = self._deps(reads, writes, extra)
        if deps.get(eng, 0) > self.cnt[eng]:
            deps[eng] = self.cnt[eng]
        if track:
            self.cnt[eng] += 1
            t = (eng, self.cnt[eng])
        else:
            t = (eng, self.cnt[eng] + 1)
        self.ops[eng].append((fn, deps, eng if track else None, 1))
        self._note(t, reads, writes)
        return t

    def dma(self, queue, fn, sem, reads=(), writes=(), extra=()):
        deps = self._deps(reads, writes, extra)
        self.dma_cnt[sem] = self.dma_cnt.get(sem, 0) + 16
        t = (sem, self.dma_cnt[sem])
        self.ops[queue].append((fn, deps, sem, 16))
        self._note(t, reads, writes)
        return t

    def fence(self):
        f = dict(self.fence_deps)
        for e in ("pe", "act", "dve", "pool"):
            f[e] = self.cnt[e]
        for s, v in self.dma_cnt.items():
            f[s] = v
        self.fence_deps = f

    def wait_all(self, eng):
        self.fence()
        self.ops[eng].append((None, dict(self.fence_deps), None, 0))

    def replay(self, eng_name, eng, sems):
        seen = {}
        for fn, deps, inc_sem, inc in self.ops[eng_name]:
            for s, v in deps.items():
                if v > 0 and seen.get(s, 0) < v:
                    eng.wait_ge(sems[s], v)
                    seen[s] = v
            if fn is None:
                continue
            ins = fn(eng)
            if inc_sem is not None:
                ins.then_inc(sems[inc_sem], inc)


def build_nc(n_layers=DEPTH, dbg=None):
    dbg = dbg or {}
    nc = bass.Bass("TRN2", target_bir_lowering=False)
    P = Plan()

    x_d = nc.dram_tensor("x", [SEQ, D], F32, kind="ExternalInput").ap()
    meta_d = nc.dram_tensor("meta", [NMETA, D], F32, kind="ExternalInput").ap()
    gains_d = nc.dram_tensor("gains", [128, 9 * KC], F32, kind="ExternalInput").ap()
    cw_d = nc.dram_tensor("cw", [128, 2 * 3 * KC], F32, kind="ExternalInput").ap()
    identf_d = nc.dram_tensor("identf", [128, 128], F32, kind="ExternalInput").ap()
    cb_d = nc.dram_tensor("cbf", [128, 6 * 128], BF16, kind="ExternalInput").ap()
    conv_w_in = nc.dram_tensor("conv_w_in", [2, D, 3 * D], F32, kind="ExternalInput").ap()
    conv_w_out = nc.dram_tensor("conv_w_out", [2, D, D], F32, kind="ExternalInput").ap()
    attn_w_qkv = nc.dram_tensor("attn_w_qkv", [2, D, 3 * D], F32, kind="ExternalInput").ap()
    attn_w_out = nc.dram_tensor("attn_w_out", [2, D, D], F32, kind="ExternalInput").ap()
    mlp_w1 = nc.dram_tensor("mlp_w1", [DEPTH, D, DFF], F32, kind="ExternalInput").ap()
    mlp_w2 = nc.dram_tensor("mlp_w2", [DEPTH, DFF, D], F32, kind="ExternalInput").ap()
    out_d = nc.dram_tensor("out", [SEQ, D], F32, kind="ExternalOutput").ap()

    uid = {"n": 0}

    def un(name):
        uid["n"] += 1
        return f"{name}_{uid['n']}"

    es = ExitStack()
    with es:
        def sb(name, shape, dt):
            return es.enter_context(nc.sbuf_tensor(name, shape, dt))

        hT = sb("hT", [128, KC, T], F32)
        hn = sb("hn", [128, KC, T], BF16)
        big2 = sb("big2", [128, KC, T], BF16)
        wring = [sb(f"wr{i}", [128, KC, 128], BF16) for i in range(N_WSLOT)]
        identf = sb("identf_s", [128, 128], F32)
        cbf = sb("cbf_s", [128, 6 * 128], BF16)
        gains = sb("gains_s", [128, 9 * KC], F32)
        cw = sb("cw_s", [128, 2 * 3 * KC], F32)
        identb = cbf[:, 0:128]
        negtri = cbf[:, 128:256]
        negones = cbf[:, 256:384]
        onesb = cbf[:, 384:512]
        m0 = cbf[:, 512:640]
        m1 = cbf[:, 640:768]

        psum = [es.enter_context(nc.psum_tensor(f"ps{i}", [128, 512], F32)) for i in range(8)]
        psum_b = [Buf() for _ in range(8)]

        NXS, NOS = 6, 4
        sem_names = ["pe", "act", "dve", "pool", "const", "metas"] + [f"xin{i}" for i in range(NXS)] + \
                    [f"outst{i}" for i in range(NOS)] + \
                    [f"w{i}" for i in range(N_WSLOT)]
        sems = {n: es.enter_context(nc.semaphore(n)) for n in sem_names}

        hT_b = [[Buf() for _ in TILES] for _ in range(KC)]
        hn_b = [[Buf() for _ in TILES] for _ in range(KC)]
        big2_b = [[Buf() for _ in TILES] for _ in range(KC)]
        wring_b = [Buf() for _ in range(N_WSLOT)]
        wstate = {"next": 0}
        rr = {"evac": 0}

        def tsl(ti):
            t0, w = TILES[ti]
            return slice(t0, t0 + w)

        P.dma("sp", lambda e: e.dma_start(out=identf[:], in_=identf_d[:, :]), "const")
        P.dma("sp", lambda e: e.dma_start(out=cbf[:], in_=cb_d[:, :]), "const")
        P.dma("sp", lambda e: e.dma_start(out=gains[:], in_=gains_d[:, :]), "const")
        P.dma("sp", lambda e: e.dma_start(out=cw[:], in_=cw_d[:, :]), "const")
        P.fence()

        wsrc = {"conv_w_in": conv_w_in, "conv_w_out": conv_w_out, "attn_w_qkv": attn_w_qkv,
                "attn_w_out": attn_w_out, "mlp_w1": mlp_w1, "mlp_w2": mlp_w2}
        wseq = []
        for l_ in range(n_layers):
            j_ = l_ // 2
            if l_ % 2 == 0:
                if not dbg.get('skip_conv'):
                    for c_ in range(KC):
                        for sec in range(3):
                            wseq.append(("conv_w_in", j_, 0, sec * D + c_ * 128, KC))
                    for cb_ in range(KC):
                        wseq.append(("conv_w_out", j_, 0, cb_ * 128, KC))
            else:
                for c_ in range(dbg.get('max_chunks', KC)):
                    for sec in range(3):
                        wseq.append(("attn_w_qkv", j_, 0, sec * D + c_ * 128, KC))
                for cb_ in range(KC):
                    wseq.append(("attn_w_out", j_, 0, cb_ * 128, KC))
            if not dbg.get('skip_mlp'):
                for g_ in range(DFF // (128 * MLP_G)):
                    for fc_ in range(MLP_G):
                        wseq.append(("mlp_w1", l_, 0, (g_ * MLP_G + fc_) * 128, KC))
                    for cb_ in range(KC):
                        wseq.append(("mlp_w2", l_, g_ * MLP_G * 128, cb_ * 128, MLP_G))
        wstate["use"] = 0
        wstate["issued"] = 0

        def issue_w(m):
            key, li, r0, c0, nk = wseq[m]
            i = m % N_WSLOT
            slot = wring[i]
            src = wsrc[key][li][r0:r0 + nk * 128, c0:c0 + 128].rearrange("(k p) c -> p k c", p=128)
            P.dma("pool", lambda e: e.dma_start(out=slot[:, 0:nk, :], in_=src), f"w{i}", writes=[wring_b[i]])

        def load_w(key, li, r0, c0, nk=KC):
            n = wstate["use"]
            assert wseq[n] == (key, li, r0, c0, nk), (n, wseq[n], (key, li, r0, c0, nk))
            wstate["use"] = n + 1
            while wstate["issued"] < min(len(wseq), n + 1 + W_PREFETCH):
                issue_w(wstate["issued"])
                wstate["issued"] += 1
            return n % N_WSLOT

        bankrr = {}

        def next_bank(banks):
            key = tuple(banks)
            i = bankrr.get(key, 0)
            bankrr[key] = i + 1
            return banks[i % len(banks)]

        def pick_evac():
            rr["evac"] ^= 1
            return "act" if rr["evac"] else "dve"

        def mm_group(bank_i, out_ap_fn, pairs, reads, first=True, last=True, track_last=True):
            n = len(pairs)
            t = None
            for idx, (l, r) in enumerate(pairs):
                st = first and idx == 0
                sp_ = last and idx == n - 1
                tr = track_last and idx == n - 1
                t = P.op("pe",
                         (lambda e, l=l, r=r, st=st, sp_=sp_: e.matmul(out_ap_fn(), lhsT=l, rhs=r, start=st, stop=sp_)),
                         reads=reads if idx == 0 else (), writes=[psum_b[bank_i]] if idx == 0 else (),
                         track=tr)
            return t

        def proj(wkey, li, r0, c0, in_buf, in_b, banks, evac, tiles=range(len(TILES)), fine=False):
            wi = load_w(wkey, li, r0, c0)
            for ti in tiles:
                bi = next_bank(banks)
                t0, w = TILES[ti]
                rd = [wring_b[wi]] + [in_b[k][ti] for k in range(KC)]
                for k in range(KC):
                    edge = (k == 0 or k == KC - 1)
                    P.op("pe", (lambda e, bi=bi, w=w, k=k, t0=t0: e.matmul(
                        psum[bi][:, 0:w], lhsT=wring[wi][:, k, :], rhs=in_buf[:, k, t0:t0 + w],
                        start=(k == 0), stop=(k == KC - 1))),
                        reads=rd if edge else (), writes=[psum_b[bi]] if edge else (), track=(k == KC - 1))
                    if fine and k % 2 == 1 and k < KC - 1:
                        yield
                evac(ti, bi)
                yield

        def run(gen):
            for _ in gen:
                pass

        def phase_input(outer=None):
            with ExitStack() as own:
                ph = own if outer is None else outer
                nxs = NXS if outer is None else 3
                xs = [ph.enter_context(nc.sbuf_tensor(un(f"xs{i}"), [128, D], F32)) for i in range(nxs)]
                ms = ph.enter_context(nc.sbuf_tensor(un("ms"), [NMETA, D], F32))
                xs_b = [Buf() for _ in range(NXS)]
                ms_b = Buf()
                P.dma("sp", lambda e: e.dma_start(out=ms[:], in_=meta_d[:, :]), "metas", writes=[ms_b])
                bi = 0
                for k in range(KC):
                    P.op("pe", (lambda e, k=k: e.transpose(psum[0][:, k * NMETA:(k + 1) * NMETA],
                                                           ms[0:NMETA, k * 128:(k + 1) * 128],
                                                           identf[0:NMETA, 0:NMETA])),
                         reads=[ms_b] if k == 0 else (), writes=[psum_b[0]] if k == 0 else (), track=(k == KC - 1))
                P.op("dve", lambda e: e.tensor_copy(out=hT[:, :, 0:NMETA],
                                                    in_=psum[0][:, 0:KC * NMETA].rearrange("p (k t) -> p k t", t=NMETA)),
                     reads=[psum_b[0]], writes=[hT_b[k][0] for k in range(KC)])
                bank_rr = 1
                for b in range(16):
                    s = b % nxs
                    P.dma("sp", (lambda e, b=b, s=s: e.dma_start(out=xs[s][:], in_=x_d[b * 128:(b + 1) * 128, :])),
                          f"xin{s}", writes=[xs_b[s]])
                    ti = 1 + b // 4
                    col = NMETA + b * 128
                    for half in range(2):
                        bi = 1 + (bank_rr % 7)
                        bank_rr += 1
                        for q in range(4):
                            k = half * 4 + q
                            P.op("pe", (lambda e, k=k, q=q, s=s, bi=bi: e.transpose(
                                psum[bi][:, q * 128:(q + 1) * 128], xs[s][:, k * 128:(k + 1) * 128], identf[:, :])),
                                reads=[xs_b[s]] if q == 0 else (), writes=[psum_b[bi]] if q == 0 else (),
                                track=(q == 3))
                        eng = pick_evac()
                        src = (lambda bi=bi: psum[bi][:, 0:512].rearrange("p (k t) -> p k t", t=128))
                        dst = (lambda half=half, col=col: hT[:, half * 4:half * 4 + 4, col:col + 128])
                        if eng == "act":
                            fn = (lambda e, src=src, dst=dst: e.activation(out=dst(), in_=src(), func=AF.Copy))
                        else:
                            fn = (lambda e, src=src, dst=dst: e.tensor_copy(out=dst(), in_=src()))
                        P.op(eng, fn, reads=[psum_b[bi]], writes=[hT_b[half * 4 + q][ti] for q in range(4)])
                if outer is None:
                    P.fence()

        def phase_norm(gidx, final=False, outer=None, alias_big2=False):
            own = ExitStack() if (outer is None and not alias_big2) else None
            ph = own if own is not None else outer
            NA = 6
            if alias_big2:
                hsq_ap = [(lambda w: big2[:, :, 0:w]), (lambda w: big2[:, :, 512:512 + w])]
                hsq_trk = [[[big2_b[k][ti] for ti in (0, 1)] for k in range(KC)],
                           [[big2_b[k][ti] for ti in (1, 2)] for k in range(KC)]]
            else:
                hsq = [ph.enter_context(nc.sbuf_tensor(un(f"hsq{i}"), [128, KC, 512], BF16)) for i in range(2)]
                hsq_ap = [(lambda w, i=i: hsq[i][:, :, 0:w]) for i in range(2)]
                hsq_trk = [[[Buf()] for k in range(KC)] for _ in range(2)]
            hsq_tr = [[b for k in range(KC) for b in hsq_trk[i][k]] for i in range(2)]
            hsq_trA = [[b for k in range(NA) for b in hsq_trk[i][k]] for i in range(2)]
            hsq_trD = [[b for k in range(NA, KC) for b in hsq_trk[i][k]] for i in range(2)]
            for ti, (t0, w) in enumerate(TILES):
                s = ti % 2
                bi = 6 + s
                P.op("act", (lambda e, s=s, t0=t0, w=w: e.activation(
                    out=hsq_ap[s](w)[:, 0:NA, :], in_=hT[:, 0:NA, t0:t0 + w], func=AF.Square)),
                    reads=[hT_b[k][ti] for k in range(NA)], writes=hsq_trA[s])
                P.op("dve", (lambda e, s=s, t0=t0, w=w: e.tensor_tensor(
                    out=hsq_ap[s](w)[:, NA:KC, :], in0=hT[:, NA:KC, t0:t0 + w], in1=hT[:, NA:KC, t0:t0 + w],
                    op=ALU.mult)),
                    reads=[hT_b[k][ti] for k in range(NA, KC)], writes=hsq_trD[s])
                for k in range(KC):
                    P.op("pe", (lambda e, s=s, bi=bi, w=w, k=k: e.matmul(
                        psum[bi][:, 0:w], lhsT=onesb, rhs=hsq_ap[s](w)[:, k, :], start=(k == 0), stop=(k == KC - 1))),
                        reads=hsq_tr[s] if k in (0, KC - 1) else (), writes=[psum_b[bi]] if k in (0, KC - 1) else (),
                        track=(k == KC - 1))
                P.op("act", (lambda e, bi=bi, w=w: e.activation(
                    out=psum[bi][:, 0:w], in_=psum[bi][:, 0:w], func=AF.Ln, scale=1.0 / D, bias=EPS)),
                    reads=[psum_b[bi]], writes=[psum_b[bi]])
                P.op("act", (lambda e, bi=bi, w=w: e.activation(
                    out=psum[bi][:, 0:w], in_=psum[bi][:, 0:w], func=AF.Exp, scale=-0.5)),
                    reads=[psum_b[bi]], writes=[psum_b[bi]])
                for k in range(KC):
                    g_ap = gains[:, gidx * KC + k:gidx * KC + k + 1]
                    if final:
                        P.op("dve", (lambda e, k=k, bi=bi, t0=t0, w=w, g_ap=g_ap: e.scalar_tensor_tensor(
                            out=hT[:, k, t0:t0 + w], in0=hT[:, k, t0:t0 + w], scalar=g_ap,
                            in1=psum[bi][:, 0:w], op0=ALU.mult, op1=ALU.mult)),
                            reads=[psum_b[bi]], writes=[hT_b[k][ti]])
                    else:
                        P.op("dve", (lambda e, k=k, bi=bi, t0=t0, w=w, g_ap=g_ap: e.scalar_tensor_tensor(
                            out=hn[:, k, t0:t0 + w], in0=hT[:, k, t0:t0 + w], scalar=g_ap,
                            in1=psum[bi][:, 0:w], op0=ALU.mult, op1=ALU.mult)),
                            reads=[psum_b[bi], hT_b[k][ti]], writes=[hn_b[k][ti]])
            if own is not None:
                P.fence()
                own.close()

        def resid_evac(cb):
            def ev(ti, bi):
                t0, w = TILES[ti]
                P.op("dve", (lambda e: e.tensor_tensor(out=hT[:, cb, t0:t0 + w], in0=psum[bi][:, 0:w],
                                                       in1=hT[:, cb, t0:t0 + w], op=ALU.add)),
                     reads=[psum_b[bi]], writes=[hT_b[cb][ti]])
            return ev

        def phase_conv(j, with_input=False):
            with ExitStack() as ph:
                vsb = [ph.enter_context(nc.sbuf_tensor(un(f"vsb{i}"), [128, 512], F32)) for i in range(2)]
                t1 = [ph.enter_context(nc.sbuf_tensor(un(f"t1{i}"), [128, 512], F32)) for i in range(2)]
                ub = [ph.enter_context(nc.sbuf_tensor(un(f"u{i}"), [128, T + 2], F32)) for i in range(2)]
                vsb_b = [Buf(), Buf()]
                t1_b = [Buf(), Buf()]
                u_b = [[Buf() for _ in TILES] for _ in range(2)]
                uh_b = [Buf(), Buf()]
                for i in range(2):
                    P.op("pool", (lambda e, i=i: e.memset(ub[i][:, 0:2], 0.0)), writes=[uh_b[i]])
                if with_input:
                    phase_input(outer=ph)
                phase_norm(0 + j, outer=ph)
                cnt = {"n": 0}
                for c in range(KC):
                    us = c % 2
                    held = {}

                    def ev_gb(ti, bi):
                        held[("gb", ti)] = bi

                    def ev_gc(ti, bi):
                        held[("gc", ti)] = bi

                    def ev_val(ti, place, c=c, us=us):
                        t0, w = TILES[ti]
                        s = cnt["n"] % 2
                        cnt["n"] += 1
                        bi, co = place
                        bgb, cgb = held[("gb", ti)]
                        bgc, cgc = held[("gc", ti)]
                        P.op("act", (lambda e: e.activation(out=vsb[s][:, 0:w], in_=psum[bi][:, co:co + w], func=AF.Copy)),
                             reads=[psum_b[bi]], writes=[vsb_b[s]])
                        P.op("dve", (lambda e: e.tensor_tensor(out=ub[us][:, 2 + t0:2 + t0 + w], in0=psum[bgc][:, cgc:cgc + w],
                                                               in1=vsb[s][:, 0:w], op=ALU.mult)),
                             reads=[psum_b[bgc], vsb_b[s]], writes=[u_b[us][ti]])
                        prev = [u_b[us][ti - 1]] if ti > 0 else [uh_b[us]]
                        base = (j * 3) * KC + c
                        w0 = cw[:, base:base + 1]
                        w1 = cw[:, base + KC:base + KC + 1]
                        w2 = cw[:, base + 2 * KC:base + 2 * KC + 1]
                        P.op("dve", (lambda e: e.tensor_scalar(out=t1[s][:, 0:w], in0=ub[us][:, t0:t0 + w],
                                                               scalar1=w0, scalar2=None, op0=ALU.mult)),
                             reads=[u_b[us][ti]] + prev, writes=[t1_b[s]])
                        P.op("dve", (lambda e: e.scalar_tensor_tensor(out=t1[s][:, 0:w], in0=ub[us][:, 1 + t0:1 + t0 + w],
                                                                      scalar=w1, in1=t1[s][:, 0:w],
                                                                      op0=ALU.mult, op1=ALU.add)),
                             reads=[u_b[us][ti]] + prev, writes=[t1_b[s]])
                        P.op("dve", (lambda e: e.scalar_tensor_tensor(out=t1[s][:, 0:w], in0=ub[us][:, 2 + t0:2 + t0 + w],
                                                                      scalar=w2, in1=t1[s][:, 0:w],
                                                                      op0=ALU.mult, op1=ALU.add)),
                             reads=[u_b[us][ti]], writes=[t1_b[s]])
                        P.op("dve", (lambda e: e.tensor_tensor(out=big2[:, c, t0:t0 + w], in0=psum[bgb][:, cgb:cgb + w],
                                                               in1=t1[s][:, 0:w], op=ALU.mult)),
                             reads=[psum_b[bgb], t1_b[s]], writes=[big2_b[c][ti]])

                    wi_gb = load_w("conv_w_in", j, 0, c * 128)
                    wi_gc = load_w("conv_w_in", j, 0, D + c * 128)
                    wi_v = load_w("conv_w_in", j, 0, 2 * D + c * 128)
                    for ti, (t0, w) in enumerate(TILES):
                        if ti == 0:
                            b0 = next_bank([6, 7])
                            places = [(b0, 0), (b0, 16), (b0, 32)]
                        else:
                            b0 = next_bank([0, 3])
                            places = [(b0, 0), (b0 + 1, 0), (b0 + 2, 0)]
                        for n_, (wi, ev) in enumerate(((wi_gb, ev_gb), (wi_gc, ev_gc), (wi_v, ev_val))):
                            bi, co = places[n_]
                            pairs = [(wring[wi][:, k, :], hn[:, k, t0:t0 + w]) for k in range(KC)]
                            mm_group(bi, (lambda bi=bi, w=w, co=co: psum[bi][:, co:co + w]), pairs,
                                     reads=[wring_b[wi]] + [hn_b[k][ti] for k in range(KC)])
                            ev(ti, (bi, co))
                for cb in range(KC):
                    run(proj("conv_w_out", j, 0, cb * 128, big2, big2_b, [0, 1, 2, 3, 4, 5], resid_evac(cb)))
                P.fence()

        def phase_mlp(l):
            with ExitStack() as ph:
                ab = [ph.enter_context(nc.sbuf_tensor(un(f"ab{i}"), [128, MLP_G, T], BF16)) for i in range(2)]
                rt = [ph.enter_context(nc.sbuf_tensor(un(f"rt{i}"), [128, 512], F32)) for i in range(2)]
                ab_b = [[[Buf() for _ in TILES] for _ in range(MLP_G)] for _ in range(2)]
                rt_b = [Buf(), Buf()]
                cnt = {"n": 0}
                phase_norm(4 + l, outer=ph)
                ngroups = DFF // (128 * MLP_G)
                for g in range(ngroups):
                    a = g % 2
                    wis = [load_w("mlp_w1", l, 0, (g * MLP_G + fc) * 128) for fc in range(MLP_G)]
                    for ti, (t0, w) in enumerate(TILES):
                        for fc in range(MLP_G):
                            wi = wis[fc]
                            bi = next_bank([0, 1, 2, 3])
                            pairs = [(wring[wi][:, k, :], hn[:, k, t0:t0 + w]) for k in range(KC)]
                            mm_group(bi, (lambda bi=bi, w=w: psum[bi][:, 0:w]), pairs,
                                     reads=[wring_b[wi]] + [hn_b[k][ti] for k in range(KC)])
                            s = cnt["n"] % 2
                            cnt["n"] += 1
                            P.op("act", (lambda e, s=s, bi=bi, w=w: e.activation(out=rt[s][:, 0:w], in_=psum[bi][:, 0:w],
                                                                                func=AF.Relu)),
                                 reads=[psum_b[bi]], writes=[rt_b[s]])
                            P.op("pool", (lambda e, s=s, a=a, fc=fc, t0=t0, w=w: e.tensor_tensor(
                                out=ab[a][:, fc, t0:t0 + w], in0=rt[s][:, 0:w], in1=rt[s][:, 0:w], op=ALU.mult)),
                                reads=[rt_b[s]], writes=[ab_b[a][fc][ti]])
                    for cb in range(KC):
                        i = load_w("mlp_w2", l, g * MLP_G * 128, cb * 128, MLP_G)
                        slot = wring[i]
                        ev2 = resid_evac(cb)
                        for ti, (t0, w) in enumerate(TILES):
                            bi = next_bank([4, 5, 6, 7])
                            pairs = [(slot[:, k, :], ab[a][:, k, t0:t0 + w]) for k in range(MLP_G)]
                            mm_group(bi, (lambda bi=bi, w=w: psum[bi][:, 0:w]), pairs,
                                     reads=[wring_b[i]] + [ab_b[a][k][ti] for k in range(MLP_G)])
                            ev2(ti, bi)
                P.fence()

        def phase_attn(j):
            with ExitStack() as ph:
                def psb(name, shape, dt):
                    return ph.enter_context(nc.sbuf_tensor(un(name), shape, dt))
                qA = [psb(f"qA{i}", [128, T], BF16) for i in range(2)]
                qB = [psb(f"qB{i}", [128, T], BF16) for i in range(2)]
                kT = [psb(f"kT{i}", [128, T], BF16) for i in range(2)]
                vA = [psb(f"vA{i}", [128, 17, 128], BF16) for i in range(2)]
                vB = [psb(f"vB{i}", [128, 17, 128], BF16) for i in range(2)]
                NS = 3
                ebuf = [psb(f"eb{i}", [128, 512], F32) for i in range(2)]
                spb = [psb(f"spb{i}", [128, 512], BF16) for i in range(NS)]
                wb = [psb(f"wb{i}", [128, 512], BF16) for i in range(NS)]
                spacc_h = [psb(f"spacc{h}", [128, 512], F32) for h in range(2)]
                spaccb_h = [[psb(f"spaccb{h}_{i}", [128, 512], BF16) for i in range(2)] for h in range(2)]
                q_b = [[Buf() for _ in TILES] for _ in range(2)]
                k_b = [[Buf() for _ in TILES] for _ in range(2)]
                v_b = [[Buf() for _ in range(5)] for _ in range(2)]
                ebuf_b = [Buf(), Buf()]
                spb_b = [Buf() for _ in range(NS)]
                wb_b = [Buf() for _ in range(NS)]
                spacc_bh = [Buf(), Buf()]
                spaccb_bh = [[Buf(), Buf()], [Buf(), Buf()]]
                for i in range(2):
                    zb_ = Buf()
                    eng = "dve" if i == 0 else "pool"
                    P.op(eng, (lambda e, i=i: e.memset(qA[i][64:128, :], 0.0)), writes=[zb_])
                    P.op(eng, (lambda e, i=i: e.memset(qB[i][0:64, :], 0.0)), writes=[zb_])
                    P.op(eng, (lambda e, i=i: e.memset(vA[i][:, :, 64:128], 0.0)), writes=[zb_])
                    P.op(eng, (lambda e, i=i: e.memset(vB[i][:, :, 0:64], 0.0)), writes=[zb_])
                    for ti in range(len(TILES)):
                        q_b[i][ti].w = zb_.w
                    for gi in range(5):
                        v_b[i][gi].w = zb_.w
                phase_norm(2 + j, alias_big2=True)
                PROJ_BANKS = [0, 1] if False else [6, 7]

                def tiles_of(ks, nk):
                    return [ti for ti, (t0, w) in enumerate(TILES) if t0 < ks + nk and ks < t0 + w]

                def qkv_proj(c):
                    s = c % 2

                    def ev_q(ti, bi):
                        t0, w = TILES[ti]
                        P.op("dve", (lambda e: e.tensor_scalar(out=qA[s][0:64, t0:t0 + w], in0=psum[bi][0:64, 0:w],
                                                               scalar1=0.125, scalar2=None, op0=ALU.mult)),
                             reads=[psum_b[bi]], writes=[q_b[s][ti]])
                        P.op("dve", (lambda e: e.tensor_scalar(out=qB[s][64:128, t0:t0 + w], in0=psum[bi][64:128, 0:w],
                                                               scalar1=0.125, scalar2=None, op0=ALU.mult)),
                             reads=[psum_b[bi]], writes=[q_b[s][ti]])

                    def ev_k(ti, bi):
                        t0, w = TILES[ti]
                        P.op("dve", (lambda e: e.tensor_copy(out=kT[s][:, t0:t0 + w], in_=psum[bi][:, 0:w])),
                             reads=[psum_b[bi]], writes=[k_b[s][ti]])
                    yield from proj("attn_w_qkv", j, 0, c * 128, hn, hn_b, PROJ_BANKS, ev_q, fine=True)
                    yield from proj("attn_w_qkv", j, 0, D + c * 128, hn, hn_b, PROJ_BANKS, ev_k, fine=True)
                    wi = load_w("attn_w_qkv", j, 0, 2 * D + c * 128)
                    groups = [[4 * g + q for q in range(4)] for g in range(4)] + [[16]]
                    for gi, blks in enumerate(groups):
                        bi = next_bank(PROJ_BANKS)
                        for q, kb in enumerate(blks):
                            ks, nk = KBLK[kb]
                            tis = tiles_of(ks, nk)
                            rd = [wring_b[wi]] + [hn_b[kk][ti] for kk in range(KC) for ti in tis]
                            for k in range(KC):
                                edge = (k == 0 or k == KC - 1)
                                P.op("pe", (lambda e, bi=bi, q=q, nk=nk, ks=ks, k=k: e.matmul(
                                    psum[bi][0:nk, q * 128:(q + 1) * 128], lhsT=hn[:, k, ks:ks + nk],
                                    rhs=wring[wi][:, k, :], start=(k == 0), stop=(k == KC - 1))),
                                    reads=rd if edge else (), writes=[psum_b[bi]] if edge else (),
                                    track=(k == KC - 1))
                                if k == 3:
                                    yield
                            yield
                        nb = len(blks)
                        nk = KBLK[blks[0]][1]
                        kb0 = blks[0]
                        src = (lambda bi=bi, nb=nb, nk=nk: psum[bi][0:nk, 0:nb * 128].rearrange("p (b c) -> p b c", c=128))
                        P.op("dve", (lambda e, src=src, nk=nk, kb0=kb0, nb=nb: e.tensor_copy(
                            out=vA[s][0:nk, kb0:kb0 + nb, 0:64], in_=src()[:, :, 0:64])),
                            reads=[psum_b[bi]], writes=[v_b[s][gi]])
                        P.op("dve", (lambda e, src=src, nk=nk, kb0=kb0, nb=nb: e.tensor_copy(
                            out=vB[s][0:nk, kb0:kb0 + nb, 64:128], in_=src()[:, :, 64:128])),
                            reads=[psum_b[bi]], writes=[v_b[s][gi]])
                        yield

                def pairs_for_chunk(c):
                    lst = []
                    blks_of = {}
                    for tj in range(dbg.get('max_tj', len(TILES))):
                        t0, W_ = TILES[tj]
                        if tj == 0:
                            blks = [dict(i=0, nk=NMETA, c0=0, N=NMETA, mask=m0)]
                        else:
                            blks = []
                            for i in range(4 * tj, -1, -1):
                                r = i - 4 * (tj - 1)
                                if r == 4:
                                    blks.append(dict(i=i, nk=NMETA, c0=496, N=16, mask=m0))
                                elif r >= 1:
                                    blks.append(dict(i=i, nk=128, c0=128 * r - 16, N=528 - 128 * r, mask=m0))
                                elif r == 0:
                                    blks.append(dict(i=i, nk=128, c0=0, N=512, mask=m1))
                                else:
                                    blks.append(dict(i=i, nk=128, c0=0, N=512, mask=None))
                        blks_of[tj] = blks
                    ntl = len(blks_of)
                    streams = []
                    for head, order in ((0, list(range(ntl))), (1, list(range(ntl - 1, -1, -1)))):
                        st = []
                        for tj in order:
                            blks = blks_of[tj]
                            for bn, b in enumerate(blks):
                                d = dict(b)
                                d.update(c=c, tj=tj, head=head, ks=KBLK[b["i"]][0],
                                         first=(bn == 0), last=(bn == len(blks) - 1))
                                d["gfirst"] = d["first"]
                                d["glast"] = d["last"]
                                d["ktis"] = tiles_of(d["ks"], d["nk"])
                                d["vg"] = b["i"] // 4
                                d["nc0"] = blks[bn + 1]["c0"] if bn + 1 < len(blks) else None
                                st.append(d)
                        streams.append(st)
                    for a_, b_ in zip(*streams):
                        lst.append(a_)
                        lst.append(b_)
                    return lst

                state = {"n": 0, "ob": 0}
                ZE = [2, 3, 4, 5]

                def stage_A(p):
                    s = p["c"] % 2
                    n = p["n"]
                    zb = ZE[n % 4]
                    ks, nk = p["ks"], p["nk"]
                    t0, _ = TILES[p["tj"]]
                    q0 = t0 + p["c0"]
                    N = p["N"]
                    qh = qA[s] if p["head"] == 0 else qB[s]
                    msk = p["mask"]
                    P.op("pe", (lambda e: e.matmul(psum[zb][0:nk, 0:N], lhsT=kT[s][:, ks:ks + nk], rhs=qh[:, q0:q0 + N],
                                                   start=True, stop=False, skip_group_check=True)),
                         reads=[k_b[s][ti] for ti in p["ktis"]] + [q_b[s][p["tj"]]], writes=[psum_b[zb]],
                         track=(msk is None))
                    if msk is not None:
                        nm = min(N, 128)
                        P.op("pe", (lambda e: e.matmul(psum[zb][0:nk, 0:nm], lhsT=identb[:, 0:nk], rhs=msk[:, 0:nm],
                                                       start=False, stop=False, skip_group_check=True)))
                    es_ = n % 2
                    P.op("act", (lambda e: e.activation(out=ebuf[es_][0:nk, 0:N], in_=psum[zb][0:nk, 0:N], func=AF.Exp)),
                         reads=[psum_b[zb]], writes=[ebuf_b[es_]])

                def stage_A2(p):
                    n = p["n"]
                    nk = p["nk"]
                    N = p["N"]
                    es_ = n % 2
                    ss_ = n % NS
                    P.op("act", (lambda e: e.activation(out=spb[ss_][0:nk, 0:N], in_=ebuf[es_][0:nk, 0:N], func=AF.Ln,
                                                        bias=1.0, scale=1.0)),
                         reads=[ebuf_b[es_]], writes=[spb_b[ss_]])

                def stage_B(p):
                    n = p["n"]
                    zb = ZE[n % 4]
                    nk = p["nk"]
                    t0, W_ = TILES[p["tj"]]
                    c0 = p["c0"]
                    N = p["N"]
                    ss_ = n % NS
                    P.op("pe", (lambda e: e.matmul(psum[zb][0:nk, 0:N], lhsT=negtri[0:nk, 0:nk], rhs=spb[ss_][0:nk, 0:N],
                                                   start=False, stop=p["first"], skip_group_check=True)),
                         reads=[spb_b[ss_]], writes=[psum_b[zb]], track=p["first"])
                    hd = p["head"]
                    spacc = spacc_h[hd]
                    spaccb = spaccb_h[hd]
                    spacc_b = spacc_bh[hd]
                    spaccb_b = spaccb_bh[hd]
                    if not p["first"]:
                        sa = p["sa"]
                        P.op("pe", (lambda e: e.matmul(psum[zb][0:nk, 0:N], lhsT=negones[:, 0:nk],
                                                       rhs=spaccb[sa][:, c0:c0 + N], start=False, stop=True,
                                                       skip_group_check=True)),
                             reads=[spaccb_b[sa]], writes=[psum_b[zb]])
                    if p["first"] and not p["last"]:
                        P.op("pool", (lambda e: e.memset(spacc[:, 0:W_], 0.0)), writes=[spacc_b])
                    if not p["last"]:
                        P.op("dve", (lambda e: e.tensor_tensor(out=spacc[0:nk, c0:W_], in0=spacc[0:nk, c0:W_],
                                                               in1=spb[ss_][0:nk, 0:N], op=ALU.add)),
                             reads=[spb_b[ss_]], writes=[spacc_b])
                        nsa = p["nsa"]
                        nc0 = p["nc0"]
                        P.op("dve", (lambda e: e.tensor_copy(out=spaccb[nsa][:, nc0:W_], in_=spacc[:, nc0:W_])),
                             reads=[spacc_b], writes=[spaccb_b[nsa]])
                    P.op("act", (lambda e: e.activation(out=wb[ss_][0:nk, 0:N], in_=psum[zb][0:nk, 0:N], func=AF.Exp)),
                         reads=[psum_b[zb]], writes=[wb_b[ss_]])

                def stage_C(p):
                    s = p["c"] % 2
                    n = p["n"]
                    nk = p["nk"]
                    t0, W_ = TILES[p["tj"]]
                    c0 = p["c0"]
                    N = p["N"]
                    ss_ = n % NS
                    ob = p["ob"]
                    vh = vA[s] if p["head"] == 0 else vB[s]
                    P.op("pe", (lambda e: e.matmul(psum[ob][:, c0:c0 + N], lhsT=vh[0:nk, p["i"], :], rhs=wb[ss_][0:nk, 0:N],
                                                   start=p["gfirst"], stop=p["glast"], skip_group_check=True)),
                         reads=[wb_b[ss_], v_b[s][p["vg"]]], writes=[psum_b[ob]] if (p["gfirst"] or p["glast"]) else (),
                         track=p["glast"])
                    if p["glast"]:
                        c = p["c"]
                        tj = p["tj"]
                        h0 = 64 * p["head"]
                        P.op("dve", (lambda e: e.tensor_copy(out=big2[h0:h0 + 64, c, t0:t0 + W_],
                                                             in_=psum[ob][h0:h0 + 64, 0:W_])),
                             reads=[psum_b[ob]], writes=[big2_b[c][tj]])

                run(qkv_proj(0))
                NCH = dbg.get('max_chunks', KC)
                for c in range(NCH):
                    nxt = qkv_proj(c + 1) if c + 1 < NCH else iter(())
                    lst = pairs_for_chunk(c)
                    for p in lst:
                        p["n"] = state["n"]
                        state["n"] += 1
                    sa_h = [0, 0]
                    for idx, p in enumerate(lst):
                        p["ob"] = p["head"]
                        hd = p["head"]
                        p["sa"] = sa_h[hd]
                        if not p["last"]:
                            sa_h[hd] ^= 1
                            p["nsa"] = sa_h[hd]
                    npairs = len(lst)
                    for step in range(npairs + 3):
                        if step < npairs:
                            stage_A(lst[step])
                        if 1 <= step <= npairs:
                            stage_A2(lst[step - 1])
                        if 2 <= step <= npairs + 1:
                            stage_B(lst[step - 2])
                        if step >= 3:
                            stage_C(lst[step - 3])
                        next(nxt, None)
                    run(nxt)
                for cb in range(KC):
                    run(proj("attn_w_out", j, 0, cb * 128, big2, big2_b, [0, 1, 2, 3, 4, 5], resid_evac(cb)))
                P.fence()

        def phase_output():
            with ExitStack() as ph:
                phase_norm(8, final=True, outer=ph)
                ost = [ph.enter_context(nc.sbuf_tensor(un(f"ost{i}"), [128, D], F32)) for i in range(NOS)]
                ost_b = [Buf() for _ in range(NOS)]
                bank_rr = 0
                for b in range(16):
                    s = b % NOS
                    ti = 1 + b // 4
                    col = NMETA + b * 128
                    for half in range(2):
                        bi = bank_rr % 8
                        bank_rr += 1
                        for q in range(4):
                            k = half * 4 + q
                            P.op("pe", (lambda e, k=k, q=q, bi=bi, col=col: e.transpose(
                                psum[bi][:, q * 128:(q + 1) * 128], hT[:, k, col:col + 128], identf[:, :])),
                                reads=[hT_b[kk][ti] for kk in range(half * 4, half * 4 + 4)] if q == 0 else (),
                                writes=[psum_b[bi]] if q == 0 else (), track=(q == 3))
                        eng = pick_evac()
                        if eng == "act":
                            fn = (lambda e, s=s, half=half, bi=bi: e.activation(
                                out=ost[s][:, half * 512:(half + 1) * 512], in_=psum[bi][:, 0:512], func=AF.Copy))
                        else:
                            fn = (lambda e, s=s, half=half, bi=bi: e.tensor_copy(
                                out=ost[s][:, half * 512:(half + 1) * 512], in_=psum[bi][:, 0:512]))
                        P.op(eng, fn, reads=[psum_b[bi]], writes=[ost_b[s]])
                    P.dma("sp", (lambda e, b=b, s=s: e.dma_start(out=out_d[b * 128:(b + 1) * 128, :], in_=ost[s][:])),
                          f"outst{s}", reads=[ost_b[s]])
                P.wait_all("sp")

        fuse_in = n_layers > 0 and not dbg.get('skip_conv')
        if not fuse_in:
            phase_input()
        for l in range(n_layers):
            j = l // 2
            if l % 2 == 0:
                if not dbg.get('skip_conv'):
                    phase_conv(j, with_input=(l == 0))
            else:
                phase_attn(j)
            if not dbg.get('skip_mlp'):
                phase_mlp(l)
        phase_output()

        with nc.Block() as block:
            @block.tensor
            def _(e):
                P.replay("pe", e, sems)

            @block.scalar
            def _(e):
                P.replay("act", e, sems)

            @block.vector
            def _(e):
                P.replay("dve", e, sems)

            @block.gpsimd
            def _(e):
                P.replay("pool", e, sems)

            @block.sync
            def _(e):
                P.replay("sp", e, sems)
    return nc, P


def _consts():
    bf = ml_dtypes.bfloat16
    ident = np.eye(128, dtype=np.float32)
    j = np.arange(128)[:, None]
    s = np.arange(128)[None, :]
    negtri = np.where(j >= s, -1.0, 0.0).astype(np.float32)
    negones = -np.ones((128, 128), np.float32)
    ones = np.ones((128, 128), np.float32)
    m0 = np.where(s <= j, NEG, 0.0).astype(np.float32)
    m1 = np.where(s <= j - 16, NEG, 0.0).astype(np.float32)
    cb = np.concatenate([ident, negtri, negones, ones, m0, m1], axis=1).astype(bf)
    return ident, cb


def _layout_vec(v):
    v = np.asarray(v, np.float32).reshape(-1, KC, 128)
    return np.ascontiguousarray(v.transpose(2, 0, 1).reshape(128, -1))


_CACHE = {}


def kernel(x, meta_tokens, conv_norm, conv_w_in, conv_w, conv_w_out, attn_norm, attn_w_qkv, attn_w_out,
           mlp_norm, mlp_w1, mlp_w2, final_norm, _n_layers=DEPTH, _dbg=None):
    x = np.asarray(x, np.float32)
    B = x.shape[0]
    key = (_n_layers, repr(_dbg))
    if key not in _CACHE:
        _CACHE[key] = build_nc(_n_layers, _dbg)[0]
    nc = _CACHE[key]
    ident, cb = _consts()
    gains = _layout_vec(np.concatenate([np.asarray(conv_norm), np.asarray(attn_norm), np.asarray(mlp_norm),
                                        np.asarray(final_norm)[None]], axis=0))
    cw = _layout_vec(np.asarray(conv_w).reshape(6, D))
    common = {
        "meta": np.ascontiguousarray(np.asarray(meta_tokens, np.float32)),
        "gains": gains, "cw": cw, "identf": ident, "cbf": cb,
        "conv_w_in": np.ascontiguousarray(np.asarray(conv_w_in, np.float32)),
        "conv_w_out": np.ascontiguousarray(np.asarray(conv_w_out, np.float32)),
        "attn_w_qkv": np.ascontiguousarray(np.asarray(attn_w_qkv, np.float32)),
        "attn_w_out": np.ascontiguousarray(np.asarray(attn_w_out, np.float32)),
        "mlp_w1": np.ascontiguousarray(np.asarray(mlp_w1, np.float32)),
        "mlp_w2": np.ascontiguousarray(np.asarray(mlp_w2, np.float32)),
    }
    in_maps = [dict(common, x=np.ascontiguousarray(x[b])) for b in range(B)]
    res = run_bass_kernel_spmd(nc, in_maps, core_ids=list(range(B)))
    return np.stack([np.asarray(r["out"], np.float32) for r in res.results], axis=0)
```

```python
from contextlib import ExitStack

import numpy as np
import ml_dtypes

import concourse.bass as bass
import concourse.mybir as mybir
from concourse.bass_utils import run_bass_kernel_spmd

F32 = mybir.dt.float32
BF16 = mybir.dt.bfloat16
AF = mybir.ActivationFunctionType
ALU = mybir.AluOpType

D = 1024
KC = 8
SEQ = 2048
NMETA = 16
T = SEQ + NMETA
DEPTH = 4
DFF = 4096
EPS = 1e-6
NEG = -30000.0
TILES = [(0, NMETA)] + [(NMETA + 512 * j, 512) for j in range(4)]
KBLK = [(128 * i, 128) for i in range(16)] + [(2048, NMETA)]
MLP_G = 4
N_WSLOT = 8
W_PREFETCH = 4


class Buf:
    __slots__ = ("w", "r")

    def __init__(self):
        self.w = None
        self.r = {}


class Plan:
    ENGS = ("pe", "act", "dve", "pool", "sp")

    def __init__(self):
        self.ops = {e: [] for e in self.ENGS}
        self.cnt = {e: 0 for e in self.ENGS}
        self.dma_cnt = {}
        self.fence_deps = {}

    def _deps(self, reads, writes, extra):
        deps = dict(self.fence_deps)

        def add(t):
            if t is None:
                return
            s, v = t
            if deps.get(s, 0) < v:
                deps[s] = v
        for t in extra:
            add(t)
        for b in reads:
            add(b.w)
        for b in writes:
            add(b.w)
            for s, v in b.r.items():
                add((s, v))
        return deps

    @staticmethod
    def _note(t, reads, writes):
        s, v = t
        for b in reads:
            if b.r.get(s, 0) < v:
                b.r[s] = v
        for b in writes:
            b.w = t
            b.r = {}

    def op(self, eng, fn, reads=(), writes=(), track=True, extra=()):
        deps = self._deps(reads, writes, extra)
        if deps.get(eng, 0) > self.cnt[eng]:
            deps[eng] = self.cnt[eng]
        if track:
            self.cnt[eng] += 1
            t = (eng, self.cnt[eng])
        else:
            t = (eng, self.cnt[eng] + 1)
        self.ops[eng].append((fn, deps, eng if track else None, 1))
        self._note(t, reads, writes)
        return t

    def dma(self, queue, fn, sem, reads=(), writes=(), extra=()):
        deps = self._deps(reads, writes, extra)
        self.dma_cnt[sem] = self.dma_cnt.get(sem, 0) + 16
        t = (sem, self.dma_cnt[sem])
        self.ops[queue].append((fn, deps, sem, 16))
        self._note(t, reads, writes)
        return t

    def fence(self):
        f = dict(self.fence_deps)
        for e in ("pe", "act", "dve", "pool"):
            f[e] = self.cnt[e]
        for s, v in self.dma_cnt.items():
            f[s] = v
        self.fence_deps = f

    def wait_all(self, eng):
        self.fence()
        self.ops[eng].append((None, dict(self.fence_deps), None, 0))

    def replay(self, eng_name, eng, sems):
        seen = {}
        for fn, deps, inc_sem, inc in self.ops[eng_name]:
            for s, v in deps.items():
                if v > 0 and seen.get(s, 0) < v:
                    eng.wait_ge(sems[s], v)
                    seen[s] = v
            if fn is None:
                continue
            ins = fn(eng)
            if inc_sem is not None:
                ins.then_inc(sems[inc_sem], inc)


def build_nc(n_layers=DEPTH, dbg=None):
    dbg = dbg or {}
    nc = bass.Bass("TRN2", target_bir_lowering=False)
    P = Plan()

    x_d = nc.dram_tensor("x", [SEQ, D], F32, kind="ExternalInput").ap()
    meta_d = nc.dram_tensor("meta", [NMETA, D], F32, kind="ExternalInput").ap()
    gains_d = nc.dram_tensor("gains", [128, 9 * KC], F32, kind="ExternalInput").ap()
    cw_d = nc.dram_tensor("cw", [128, 2 * 3 * KC], F32, kind="ExternalInput").ap()
    identf_d = nc.dram_tensor("identf", [128, 128], F32, kind="ExternalInput").ap()
    cb_d = nc.dram_tensor("cbf", [128, 6 * 128], BF16, kind="ExternalInput").ap()
    conv_w_in = nc.dram_tensor("conv_w_in", [2, D, 3 * D], F32, kind="ExternalInput").ap()
    conv_w_out = nc.dram_tensor("conv_w_out", [2, D, D], F32, kind="ExternalInput").ap()
    attn_w_qkv = nc.dram_tensor("attn_w_qkv", [2, D, 3 * D], F32, kind="ExternalInput").ap()
    attn_w_out = nc.dram_tensor("attn_w_out", [2, D, D], F32, kind="ExternalInput").ap()
    mlp_w1 = nc.dram_tensor("mlp_w1", [DEPTH, D, DFF], F32, kind="ExternalInput").ap()
    mlp_w2 = nc.dram_tensor("mlp_w2", [DEPTH, DFF, D], F32, kind="ExternalInput").ap()
    out_d = nc.dram_tensor("out", [SEQ, D], F32, kind="ExternalOutput").ap()

    uid = {"n": 0}

    def un(name):
        uid["n"] += 1
        return f"{name}_{uid['n']}"

    es = ExitStack()
    with es:
        def sb(name, shape, dt):
            return es.enter_context(nc.sbuf_tensor(name, shape, dt))

        hT = sb("hT", [128, KC, T], F32)
        hn = sb("hn", [128, KC, T], BF16)
        big2 = sb("big2", [128, KC, T], BF16)
        wring = [sb(f"wr{i}", [128, KC, 128], BF16) for i in range(N_WSLOT)]
        identf = sb("identf_s", [128, 128], F32)
        cbf = sb("cbf_s", [128, 6 * 128], BF16)
        gains = sb("gains_s", [128, 9 * KC], F32)
        cw = sb("cw_s", [128, 2 * 3 * KC], F32)
        identb = cbf[:, 0:128]
        negtri = cbf[:, 128:256]
        negones = cbf[:, 256:384]
        onesb = cbf[:, 384:512]
        m0 = cbf[:, 512:640]
        m1 = cbf[:, 640:768]

        psum = [es.enter_context(nc.psum_tensor(f"ps{i}", [128, 512], F32)) for i in range(8)]
        psum_b = [Buf() for _ in range(8)]

        NXS, NOS = 6, 4
        sem_names = ["pe", "act", "dve", "pool", "const", "metas"] + [f"xin{i}" for i in range(NXS)] + \
                    [f"outst{i}" for i in range(NOS)] + \
                    [f"w{i}" for i in range(N_WSLOT)]
        sems = {n: es.enter_context(nc.semaphore(n)) for n in sem_names}

        hT_b = [[Buf() for _ in TILES] for _ in range(KC)]
        hn_b = [[Buf() for _ in TILES] for _ in range(KC)]
        big2_b = [[Buf() for _ in TILES] for _ in range(KC)]
        wring_b = [Buf() for _ in range(N_WSLOT)]
        wstate = {"next": 0}
        rr = {"evac": 0}

        def tsl(ti):
            t0, w = TILES[ti]
            return slice(t0, t0 + w)

        P.dma("sp", lambda e: e.dma_start(out=identf[:], in_=identf_d[:, :]), "const")
        P.dma("sp", lambda e: e.dma_start(out=cbf[:], in_=cb_d[:, :]), "const")
        P.dma("sp", lambda e: e.dma_start(out=gains[:], in_=gains_d[:, :]), "const")
        P.dma("sp", lambda e: e.dma_start(out=cw[:], in_=cw_d[:, :]), "const")
        P.fence()

        wsrc = {"conv_w_in": conv_w_in, "conv_w_out": conv_w_out, "attn_w_qkv": attn_w_qkv,
                "attn_w_out": attn_w_out, "mlp_w1": mlp_w1, "mlp_w2": mlp_w2}
        wseq = []
        for l_ in range(n_layers):
            j_ = l_ // 2
            if l_ % 2 == 0:
                if not dbg.get('skip_conv'):
                    for c_ in range(KC):
                        for sec in range(3):
                            wseq.append(("conv_w_in", j_, 0, sec * D + c_ * 128, KC))
                    for cb_ in range(KC):
                        wseq.append(("conv_w_out", j_, 0, cb_ * 128, KC))
            else:
                for c_ in range(dbg.get('max_chunks', KC)):
                    for sec in range(3):
                        wseq.append(("attn_w_qkv", j_, 0, sec * D + c_ * 128, KC))
                for cb_ in range(KC):
                    wseq.append(("attn_w_out", j_, 0, cb_ * 128, KC))
            if not dbg.get('skip_mlp'):
                for g_ in range(DFF // (128 * MLP_G)):
                    for fc_ in range(MLP_G):
                        wseq.append(("mlp_w1", l_, 0, (g_ * MLP_G + fc_) * 128, KC))
                    for cb_ in range(KC):
                        wseq.append(("mlp_w2", l_, g_ * MLP_G * 128, cb_ * 128, MLP_G))
        wstate["use"] = 0
        wstate["issued"] = 0

        def issue_w(m):
            key, li, r0, c0, nk = wseq[m]
            i = m % N_WSLOT
            slot = wring[i]
            src = wsrc[key][li][r0:r0 + nk * 128, c0:c0 + 128].rearrange("(k p) c -> p k c", p=128)
            P.dma("pool", lambda e: e.dma_start(out=slot[:, 0:nk, :], in_=src), f"w{i}", writes=[wring_b[i]])

        def load_w(key, li, r0, c0, nk=KC):
            n = wstate["use"]
            assert wseq[n] == (key, li, r0, c0, nk), (n, wseq[n], (key, li, r0, c0, nk))
            wstate["use"] = n + 1
            while wstate["issued"] < min(len(wseq), n + 1 + W_PREFETCH):
                issue_w(wstate["issued"])
                wstate["issued"] += 1
            return n % N_WSLOT

        bankrr = {}

        def next_bank(banks):
            key = tuple(banks)
            i = bankrr.get(key, 0)
            bankrr[key] = i + 1
            return banks[i % len(banks)]

        def pick_evac():
            rr["evac"] ^= 1
            return "act" if rr["evac"] else "dve"

        def mm_group(bank_i, out_ap_fn, pairs, reads, first=True, last=True, track_last=True):
            n = len(pairs)
            t = None
            for idx, (l, r) in enumerate(pairs):
                st = first and idx == 0
                sp_ = last and idx == n - 1
                tr = track_last and idx == n - 1
                t = P.op("pe",
                         (lambda e, l=l, r=r, st=st, sp_=sp_: e.matmul(out_ap_fn(), lhsT=l, rhs=r, start=st, stop=sp_)),
                         reads=reads if idx == 0 else (), writes=[psum_b[bank_i]] if idx == 0 else (),
                         track=tr)
            return t

        def proj(wkey, li, r0, c0, in_buf, in_b, banks, evac, tiles=range(len(TILES)), fine=False):
            wi = load_w(wkey, li, r0, c0)
            for ti in tiles:
                bi = next_bank(banks)
                t0, w = TILES[ti]
                rd = [wring_b[wi]] + [in_b[k][ti] for k in range(KC)]
                for k in range(KC):
                    edge = (k == 0 or k == KC - 1)
                    P.op("pe", (lambda e, bi=bi, w=w, k=k, t0=t0: e.matmul(
                        psum[bi][:, 0:w], lhsT=wring[wi][:, k, :], rhs=in_buf[:, k, t0:t0 + w],
                        start=(k == 0), stop=(k == KC - 1))),
                        reads=rd if edge else (), writes=[psum_b[bi]] if edge else (), track=(k == KC - 1))
                    if fine and k % 2 == 1 and k < KC - 1:
                        yield
                evac(ti, bi)
                yield

        def run(gen):
            for _ in gen:
                pass

        def phase_input(outer=None):
            with ExitStack() as own:
                ph = own if outer is None else outer
                nxs = NXS if outer is None else 3
                xs = [ph.enter_context(nc.sbuf_tensor(un(f"xs{i}"), [128, D], F32)) for i in range(nxs)]
                ms = ph.enter_context(nc.sbuf_tensor(un("ms"), [NMETA, D], F32))
                xs_b = [Buf() for _ in range(NXS)]
                ms_b = Buf()
                P.dma("sp", lambda e: e.dma_start(out=ms[:], in_=meta_d[:, :]), "metas", writes=[ms_b])
                bi = 0
                for k in range(KC):
                    P.op("pe", (lambda e, k=k: e.transpose(psum[0][:, k * NMETA:(k + 1) * NMETA],
                                                           ms[0:NMETA, k * 128:(k + 1) * 128],
                                                           identf[0:NMETA, 0:NMETA])),
                         reads=[ms_b] if k == 0 else (), writes=[psum_b[0]] if k == 0 else (), track=(k == KC - 1))
                P.op("dve", lambda e: e.tensor_copy(out=hT[:, :, 0:NMETA],
                                                    in_=psum[0][:, 0:KC * NMETA].rearrange("p (k t) -> p k t", t=NMETA)),
                     reads=[psum_b[0]], writes=[hT_b[k][0] for k in range(KC)])
                bank_rr = 1
                for b in range(16):
                    s = b % nxs
                    P.dma("sp", (lambda e, b=b, s=s: e.dma_start(out=xs[s][:], in_=x_d[b * 128:(b + 1) * 128, :])),
                          f"xin{s}", writes=[xs_b[s]])
                    ti = 1 + b // 4
                    col = NMETA + b * 128
                    for half in range(2):
                        bi = 1 + (bank_rr % 7)
                        bank_rr += 1
                        for q in range(4):
                            k = half * 4 + q
                            P.op("pe", (lambda e, k=k, q=q, s=s, bi=bi: e.transpose(
                                psum[bi][:, q * 128:(q + 1) * 128], xs[s][:, k * 128:(k + 1) * 128], identf[:, :])),
                                reads=[xs_b[s]] if q == 0 else (), writes=[psum_b[bi]] if q == 0 else (),
                                track=(q == 3))
                        eng = pick_evac()
                        src = (lambda bi=bi: psum[bi][:, 0:512].rearrange("p (k t) -> p k t", t=128))
                        dst = (lambda half=half, col=col: hT[:, half * 4:half * 4 + 4, col:col + 128])
                        if eng == "act":
                            fn = (lambda e, src=src, dst=dst: e.activation(out=dst(), in_=src(), func=AF.Copy))
                        else:
                            fn = (lambda e, src=src, dst=dst: e.tensor_copy(out=dst(), in_=src()))
                        P.op(eng, fn, reads=[psum_b[bi]], writes=[hT_b[half * 4 + q][ti] for q in range(4)])
                if outer is None:
                    P.fence()

        def phase_norm(gidx, final=False, outer=None, alias_big2=False):
            own = ExitStack() if (outer is None and not alias_big2) else None
            ph = own if own is not None else outer
            if alias_big2:
                hsq_ap = [(lambda w: big2[:, :, 0:w]), (lambda w: big2[:, :, 512:512 + w])]
                hsq_tr = [[big2_b[k][ti] for k in range(KC) for ti in (0, 1)],
                          [big2_b[k][ti] for k in range(KC) for ti in (1, 2)]]
            else:
                hsq = [ph.enter_context(nc.sbuf_tensor(un(f"hsq{i}"), [128, KC, 512], BF16)) for i in range(2)]
                hsq_ap = [(lambda w, i=i: hsq[i][:, :, 0:w]) for i in range(2)]
                hsq_tr = [[Buf()], [Buf()]]
            def sq(ti):
                t0, w = TILES[ti]
                s = ti % 2
                P.op("act", (lambda e, s=s, t0=t0, w=w: e.activation(
                    out=hsq_ap[s](w), in_=hT[:, :, t0:t0 + w], func=AF.Square)),
                    reads=[hT_b[k][ti] for k in range(KC)], writes=hsq_tr[s])

            sq(0)
            for ti, (t0, w) in enumerate(TILES):
                s = ti % 2
                bi = 6 + s
                if ti + 1 < len(TILES):
                    sq(ti + 1)
                for k in range(KC):
                    P.op("pe", (lambda e, s=s, bi=bi, w=w, k=k: e.matmul(
                        psum[bi][:, 0:w], lhsT=onesb, rhs=hsq_ap[s](w)[:, k, :], start=(k == 0), stop=(k == KC - 1))),
                        reads=hsq_tr[s] if k in (0, KC - 1) else (), writes=[psum_b[bi]] if k in (0, KC - 1) else (),
                        track=(k == KC - 1))
                P.op("act", (lambda e, bi=bi, w=w: e.activation(
                    out=psum[bi][:, 0:w], in_=psum[bi][:, 0:w], func=AF.Ln, scale=1.0 / D, bias=EPS)),
                    reads=[psum_b[bi]], writes=[psum_b[bi]])
                P.op("act", (lambda e, bi=bi, w=w: e.activation(
                    out=psum[bi][:, 0:w], in_=psum[bi][:, 0:w], func=AF.Exp, scale=-0.5)),
                    reads=[psum_b[bi]], writes=[psum_b[bi]])
                for k in range(KC):
                    g_ap = gains[:, gidx * KC + k:gidx * KC + k + 1]
                    if final:
                        P.op("dve", (lambda e, k=k, bi=bi, t0=t0, w=w, g_ap=g_ap: e.scalar_tensor_tensor(
                            out=hT[:, k, t0:t0 + w], in0=hT[:, k, t0:t0 + w], scalar=g_ap,
                            in1=psum[bi][:, 0:w], op0=ALU.mult, op1=ALU.mult)),
                            reads=[psum_b[bi]], writes=[hT_b[k][ti]])
                    else:
                        P.op("dve", (lambda e, k=k, bi=bi, t0=t0, w=w, g_ap=g_ap: e.scalar_tensor_tensor(
                            out=hn[:, k, t0:t0 + w], in0=hT[:, k, t0:t0 + w], scalar=g_ap,
                            in1=psum[bi][:, 0:w], op0=ALU.mult, op1=ALU.mult)),
                            reads=[psum_b[bi], hT_b[k][ti]], writes=[hn_b[k][ti]])
            if own is not None:
                P.fence()
                own.close()

        def resid_evac(cb):
            def ev(ti, bi):
                t0, w = TILES[ti]
                P.op("dve", (lambda e: e.tensor_tensor(out=hT[:, cb, t0:t0 + w], in0=psum[bi][:, 0:w],
                                                       in1=hT[:, cb, t0:t0 + w], op=ALU.add)),
                     reads=[psum_b[bi]], writes=[hT_b[cb][ti]])
            return ev

        def phase_conv(j, with_input=False):
            with ExitStack() as ph:
                vsb = [ph.enter_context(nc.sbuf_tensor(un(f"vsb{i}"), [128, 512], F32)) for i in range(2)]
                t1 = [ph.enter_context(nc.sbuf_tensor(un(f"t1{i}"), [128, 512], F32)) for i in range(2)]
                ub = [ph.enter_context(nc.sbuf_tensor(un(f"u{i}"), [128, T + 2], F32)) for i in range(2)]
                vsb_b = [Buf(), Buf()]
                t1_b = [Buf(), Buf()]
                u_b = [[Buf() for _ in TILES] for _ in range(2)]
                uh_b = [Buf(), Buf()]
                for i in range(2):
                    P.op("pool", (lambda e, i=i: e.memset(ub[i][:, 0:2], 0.0)), writes=[uh_b[i]])
                if with_input:
                    phase_input(outer=ph)
                phase_norm(0 + j, outer=ph)
                cnt = {"n": 0}
                for c in range(KC):
                    us = c % 2
                    held = {}

                    def ev_gb(ti, bi):
                        held[("gb", ti)] = bi

                    def ev_gc(ti, bi):
                        held[("gc", ti)] = bi

                    def ev_val(ti, place, c=c, us=us):
                        t0, w = TILES[ti]
                        s = cnt["n"] % 2
                        cnt["n"] += 1
                        bi, co = place
                        bgb, cgb = held[("gb", ti)]
                        bgc, cgc = held[("gc", ti)]
                        P.op("act", (lambda e: e.activation(out=vsb[s][:, 0:w], in_=psum[bi][:, co:co + w], func=AF.Copy)),
                             reads=[psum_b[bi]], writes=[vsb_b[s]])
                        P.op("dve", (lambda e: e.tensor_tensor(out=ub[us][:, 2 + t0:2 + t0 + w], in0=psum[bgc][:, cgc:cgc + w],
                                                               in1=vsb[s][:, 0:w], op=ALU.mult)),
                             reads=[psum_b[bgc], vsb_b[s]], writes=[u_b[us][ti]])
                        prev = [u_b[us][ti - 1]] if ti > 0 else [uh_b[us]]
                        base = (j * 3) * KC + c
                        w0 = cw[:, base:base + 1]
                        w1 = cw[:, base + KC:base + KC + 1]
                        w2 = cw[:, base + 2 * KC:base + 2 * KC + 1]
                        P.op("dve", (lambda e: e.tensor_scalar(out=t1[s][:, 0:w], in0=ub[us][:, t0:t0 + w],
                                                               scalar1=w0, scalar2=None, op0=ALU.mult)),
                             reads=[u_b[us][ti]] + prev, writes=[t1_b[s]])
                        P.op("dve", (lambda e: e.scalar_tensor_tensor(out=t1[s][:, 0:w], in0=ub[us][:, 1 + t0:1 + t0 + w],
                                                                      scalar=w1, in1=t1[s][:, 0:w],
                                                                      op0=ALU.mult, op1=ALU.add)),
                             reads=[u_b[us][ti]] + prev, writes=[t1_b[s]])
                        P.op("dve", (lambda e: e.scalar_tensor_tensor(out=t1[s][:, 0:w], in0=ub[us][:, 2 + t0:2 + t0 + w],
                                                                      scalar=w2, in1=t1[s][:, 0:w],
                                                                      op0=ALU.mult, op1=ALU.add)),
                             reads=[u_b[us][ti]], writes=[t1_b[s]])
                        P.op("dve", (lambda e: e.tensor_tensor(out=big2[:, c, t0:t0 + w], in0=psum[bgb][:, cgb:cgb + w],
                                                               in1=t1[s][:, 0:w], op=ALU.mult)),
                             reads=[psum_b[bgb], t1_b[s]], writes=[big2_b[c][ti]])

                    wi_gb = load_w("conv_w_in", j, 0, c * 128)
                    wi_gc = load_w("conv_w_in", j, 0, D + c * 128)
                    wi_v = load_w("conv_w_in", j, 0, 2 * D + c * 128)
                    for ti, (t0, w) in enumerate(TILES):
                        if ti == 0:
                            b0 = next_bank([6, 7])
                            places = [(b0, 0), (b0, 16), (b0, 32)]
                        else:
                            b0 = next_bank([0, 3])
                            places = [(b0, 0), (b0 + 1, 0), (b0 + 2, 0)]
                        for n_, (wi, ev) in enumerate(((wi_gb, ev_gb), (wi_gc, ev_gc), (wi_v, ev_val))):
                            bi, co = places[n_]
                            pairs = [(wring[wi][:, k, :], hn[:, k, t0:t0 + w]) for k in range(KC)]
                            mm_group(bi, (lambda bi=bi, w=w, co=co: psum[bi][:, co:co + w]), pairs,
                                     reads=[wring_b[wi]] + [hn_b[k][ti] for k in range(KC)])
                            ev(ti, (bi, co))
                for cb in range(KC):
                    run(proj("conv_w_out", j, 0, cb * 128, big2, big2_b, [0, 1, 2, 3, 4, 5], resid_evac(cb)))
                P.fence()

        def phase_mlp(l):
            with ExitStack() as ph:
                ab = [ph.enter_context(nc.sbuf_tensor(un(f"ab{i}"), [128, MLP_G, T], BF16)) for i in range(2)]
                rt = [ph.enter_context(nc.sbuf_tensor(un(f"rt{i}"), [128, 512], F32)) for i in range(2)]
                ab_b = [[[Buf() for _ in TILES] for _ in range(MLP_G)] for _ in range(2)]
                rt_b = [Buf(), Buf()]
                cnt = {"n": 0}
                phase_norm(4 + l, outer=ph)
                ngroups = DFF // (128 * MLP_G)
                for g in range(ngroups):
                    a = g % 2
                    wis = [load_w("mlp_w1", l, 0, (g * MLP_G + fc) * 128) for fc in range(MLP_G)]
                    for ti, (t0, w) in enumerate(TILES):
                        for fc in range(MLP_G):
                            wi = wis[fc]
                            bi = next_bank([0, 1, 2, 3])
                            pairs = [(wring[wi][:, k, :], hn[:, k, t0:t0 + w]) for k in range(KC)]
                            mm_group(bi, (lambda bi=bi, w=w: psum[bi][:, 0:w]), pairs,
                                     reads=[wring_b[wi]] + [hn_b[k][ti] for k in range(KC)])
                            s = cnt["n"] % 2
                            cnt["n"] += 1
                            P.op("act", (lambda e, s=s, bi=bi, w=w: e.activation(out=rt[s][:, 0:w], in_=psum[bi][:, 0:w],
                                                                                func=AF.Relu)),
                                 reads=[psum_b[bi]], writes=[rt_b[s]])
                            P.op("pool", (lambda e, s=s, a=a, fc=fc, t0=t0, w=w: e.tensor_tensor(
                                out=ab[a][:, fc, t0:t0 + w], in0=rt[s][:, 0:w], in1=rt[s][:, 0:w], op=ALU.mult)),
                                reads=[rt_b[s]], writes=[ab_b[a][fc][ti]])
                    for cb in range(KC):
                        i = load_w("mlp_w2", l, g * MLP_G * 128, cb * 128, MLP_G)
                        slot = wring[i]
                        ev2 = resid_evac(cb)
                        for ti, (t0, w) in enumerate(TILES):
                            bi = next_bank([4, 5, 6, 7])
                            pairs = [(slot[:, k, :], ab[a][:, k, t0:t0 + w]) for k in range(MLP_G)]
                            mm_group(bi, (lambda bi=bi, w=w: psum[bi][:, 0:w]), pairs,
                                     reads=[wring_b[i]] + [ab_b[a][k][ti] for k in range(MLP_G)])
                            ev2(ti, bi)
                P.fence()

        def phase_attn(j):
            with ExitStack() as ph:
                def psb(name, shape, dt):
                    return ph.enter_context(nc.sbuf_tensor(un(name), shape, dt))
                qA = [psb(f"qA{i}", [128, T], BF16) for i in range(2)]
                qB = [psb(f"qB{i}", [128, T], BF16) for i in range(2)]
                kT = [psb(f"kT{i}", [128, T], BF16) for i in range(2)]
                vA = [psb(f"vA{i}", [128, 17, 128], BF16) for i in range(2)]
                vB = [psb(f"vB{i}", [128, 17, 128], BF16) for i in range(2)]
                NS = 3
                ebuf = [psb(f"eb{i}", [128, 512], F32) for i in range(2)]
                spb = [psb(f"spb{i}", [128, 512], BF16) for i in range(NS)]
                wb = [psb(f"wb{i}", [128, 512], BF16) for i in range(NS)]
                spacc_h = [psb(f"spacc{h}", [128, 512], F32) for h in range(2)]
                spaccb_h = [[psb(f"spaccb{h}_{i}", [128, 512], BF16) for i in range(2)] for h in range(2)]
                q_b = [[Buf() for _ in TILES] for _ in range(2)]
                k_b = [[Buf() for _ in TILES] for _ in range(2)]
                v_b = [[Buf() for _ in range(5)] for _ in range(2)]
                ebuf_b = [Buf(), Buf()]
                spb_b = [Buf() for _ in range(NS)]
                wb_b = [Buf() for _ in range(NS)]
                spacc_bh = [Buf(), Buf()]
                spaccb_bh = [[Buf(), Buf()], [Buf(), Buf()]]
                for i in range(2):
                    zb_ = Buf()
                    eng = "dve" if i == 0 else "pool"
                    P.op(eng, (lambda e, i=i: e.memset(qA[i][64:128, :], 0.0)), writes=[zb_])
                    P.op(eng, (lambda e, i=i: e.memset(qB[i][0:64, :], 0.0)), writes=[zb_])
                    P.op(eng, (lambda e, i=i: e.memset(vA[i][:, :, 64:128], 0.0)), writes=[zb_])
                    P.op(eng, (lambda e, i=i: e.memset(vB[i][:, :, 0:64], 0.0)), writes=[zb_])
                    for ti in range(len(TILES)):
                        q_b[i][ti].w = zb_.w
                    for gi in range(5):
                        v_b[i][gi].w = zb_.w
                phase_norm(2 + j, alias_big2=True)
                PROJ_BANKS = [0, 1] if False else [6, 7]

                def tiles_of(ks, nk):
                    return [ti for ti, (t0, w) in enumerate(TILES) if t0 < ks + nk and ks < t0 + w]

                def qkv_proj(c):
                    s = c % 2

                    def ev_q(ti, bi):
                        t0, w = TILES[ti]
                        P.op("dve", (lambda e: e.tensor_scalar(out=qA[s][0:64, t0:t0 + w], in0=psum[bi][0:64, 0:w],
                                                               scalar1=0.125, scalar2=None, op0=ALU.mult)),
                             reads=[psum_b[bi]], writes=[q_b[s][ti]])
                        P.op("dve", (lambda e: e.tensor_scalar(out=qB[s][64:128, t0:t0 + w], in0=psum[bi][64:128, 0:w],
                                                               scalar1=0.125, scalar2=None, op0=ALU.mult)),
                             reads=[psum_b[bi]], writes=[q_b[s][ti]])

                    def ev_k(ti, bi):
                        t0, w = TILES[ti]
                        P.op("dve", (lambda e: e.tensor_copy(out=kT[s][:, t0:t0 + w], in_=psum[bi][:, 0:w])),
                             reads=[psum_b[bi]], writes=[k_b[s][ti]])
                    yield from proj("attn_w_qkv", j, 0, c * 128, hn, hn_b, PROJ_BANKS, ev_q, fine=True)
                    yield from proj("attn_w_qkv", j, 0, D + c * 128, hn, hn_b, PROJ_BANKS, ev_k, fine=True)
                    wi = load_w("attn_w_qkv", j, 0, 2 * D + c * 128)
                    groups = [[4 * g + q for q in range(4)] for g in range(4)] + [[16]]
                    for gi, blks in enumerate(groups):
                        bi = next_bank(PROJ_BANKS)
                        for q, kb in enumerate(blks):
                            ks, nk = KBLK[kb]
                            tis = tiles_of(ks, nk)
                            rd = [wring_b[wi]] + [hn_b[kk][ti] for kk in range(KC) for ti in tis]
                            for k in range(KC):
                                edge = (k == 0 or k == KC - 1)
                                P.op("pe", (lambda e, bi=bi, q=q, nk=nk, ks=ks, k=k: e.matmul(
                                    psum[bi][0:nk, q * 128:(q + 1) * 128], lhsT=hn[:, k, ks:ks + nk],
                                    rhs=wring[wi][:, k, :], start=(k == 0), stop=(k == KC - 1))),
                                    reads=rd if edge else (), writes=[psum_b[bi]] if edge else (),
                                    track=(k == KC - 1))
                                if k == 3:
                                    yield
                            yield
                        nb = len(blks)
                        nk = KBLK[blks[0]][1]
                        kb0 = blks[0]
                        src = (lambda bi=bi, nb=nb, nk=nk: psum[bi][0:nk, 0:nb * 128].rearrange("p (b c) -> p b c", c=128))
                        P.op("dve", (lambda e, src=src, nk=nk, kb0=kb0, nb=nb: e.tensor_copy(
                            out=vA[s][0:nk, kb0:kb0 + nb, 0:64], in_=src()[:, :, 0:64])),
                            reads=[psum_b[bi]], writes=[v_b[s][gi]])
                        P.op("dve", (lambda e, src=src, nk=nk, kb0=kb0, nb=nb: e.tensor_copy(
                            out=vB[s][0:nk, kb0:kb0 + nb, 64:128], in_=src()[:, :, 64:128])),
                            reads=[psum_b[bi]], writes=[v_b[s][gi]])
                        yield

                def pairs_for_chunk(c):
                    lst = []
                    blks_of = {}
                    for tj in range(dbg.get('max_tj', len(TILES))):
                        t0, W_ = TILES[tj]
                        if tj == 0:
                            blks = [dict(i=0, nk=NMETA, c0=0, N=NMETA, mask=m0)]
                        else:
                            blks = []
                            for i in range(4 * tj, -1, -1):
                                r = i - 4 * (tj - 1)
                                if r == 4:
                                    blks.append(dict(i=i, nk=NMETA, c0=496, N=16, mask=m0))
                                elif r >= 1:
                                    blks.append(dict(i=i, nk=128, c0=128 * r - 16, N=528 - 128 * r, mask=m0))
                                elif r == 0:
                                    blks.append(dict(i=i, nk=128, c0=0, N=512, mask=m1))
                                else:
                                    blks.append(dict(i=i, nk=128, c0=0, N=512, mask=None))
                        blks_of[tj] = blks
                    ntl = len(blks_of)
                    streams = []
                    for head, order in ((0, list(range(ntl))), (1, list(range(ntl - 1, -1, -1)))):
                        st = []
                        for tj in order:
                            blks = blks_of[tj]
                            for bn, b in enumerate(blks):
                                d = dict(b)
                                d.update(c=c, tj=tj, head=head, ks=KBLK[b["i"]][0],
                                         first=(bn == 0), last=(bn == len(blks) - 1))
                                d["gfirst"] = d["first"]
                                d["glast"] = d["last"]
                                d["ktis"] = tiles_of(d["ks"], d["nk"])
                                d["vg"] = b["i"] // 4
                                d["nc0"] = blks[bn + 1]["c0"] if bn + 1 < len(blks) else None
                                st.append(d)
                        streams.append(st)
                    for a_, b_ in zip(*streams):
                        lst.append(a_)
                        lst.append(b_)
                    return lst

                state = {"n": 0, "ob": 0}
                ZE = [2, 3, 4, 5]

                def stage_A(p):
                    s = p["c"] % 2
                    n = p["n"]
                    zb = ZE[n % 4]
                    ks, nk = p["ks"], p["nk"]
                    t0, _ = TILES[p["tj"]]
                    q0 = t0 + p["c0"]
                    N = p["N"]
                    qh = qA[s] if p["head"] == 0 else qB[s]
                    msk = p["mask"]
                    P.op("pe", (lambda e: e.matmul(psum[zb][0:nk, 0:N], lhsT=kT[s][:, ks:ks + nk], rhs=qh[:, q0:q0 + N],
                                                   start=True, stop=False, skip_group_check=True)),
                         reads=[k_b[s][ti] for ti in p["ktis"]] + [q_b[s][p["tj"]]], writes=[psum_b[zb]],
                         track=(msk is None))
                    if msk is not None:
                        nm = min(N, 128)
                        P.op("pe", (lambda e: e.matmul(psum[zb][0:nk, 0:nm], lhsT=identb[:, 0:nk], rhs=msk[:, 0:nm],
                                                       start=False, stop=False, skip_group_check=True)))
                    es_ = n % 2
                    P.op("act", (lambda e: e.activation(out=ebuf[es_][0:nk, 0:N], in_=psum[zb][0:nk, 0:N], func=AF.Exp)),
                         reads=[psum_b[zb]], writes=[ebuf_b[es_]])

                def stage_A2(p):
                    n = p["n"]
                    nk = p["nk"]
                    N = p["N"]
                    es_ = n % 2
                    ss_ = n % NS
                    P.op("act", (lambda e: e.activation(out=spb[ss_][0:nk, 0:N], in_=ebuf[es_][0:nk, 0:N], func=AF.Ln,
                                                        bias=1.0, scale=1.0)),
                         reads=[ebuf_b[es_]], writes=[spb_b[ss_]])

                def stage_B(p):
                    n = p["n"]
                    zb = ZE[n % 4]
                    nk = p["nk"]
                    t0, W_ = TILES[p["tj"]]
                    c0 = p["c0"]
                    N = p["N"]
                    ss_ = n % NS
                    P.op("pe", (lambda e: e.matmul(psum[zb][0:nk, 0:N], lhsT=negtri[0:nk, 0:nk], rhs=spb[ss_][0:nk, 0:N],
                                                   start=False, stop=p["first"], skip_group_check=True)),
                         reads=[spb_b[ss_]], writes=[psum_b[zb]], track=p["first"])
                    hd = p["head"]
                    spacc = spacc_h[hd]
                    spaccb = spaccb_h[hd]
                    spacc_b = spacc_bh[hd]
                    spaccb_b = spaccb_bh[hd]
                    if not p["first"]:
                        sa = p["sa"]
                        P.op("pe", (lambda e: e.matmul(psum[zb][0:nk, 0:N], lhsT=negones[:, 0:nk],
                                                       rhs=spaccb[sa][:, c0:c0 + N], start=False, stop=True,
                                                       skip_group_check=True)),
                             reads=[spaccb_b[sa]], writes=[psum_b[zb]])
                    if p["first"] and not p["last"]:
                        P.op("pool", (lambda e: e.memset(spacc[:, 0:W_], 0.0)), writes=[spacc_b])
                    if not p["last"]:
                        P.op("dve", (lambda e: e.tensor_tensor(out=spacc[0:nk, c0:W_], in0=spacc[0:nk, c0:W_],
                                                               in1=spb[ss_][0:nk, 0:N], op=ALU.add)),
                             reads=[spb_b[ss_]], writes=[spacc_b])
                        nsa = p["nsa"]
                        nc0 = p["nc0"]
                        P.op("dve", (lambda e: e.tensor_copy(out=spaccb[nsa][:, nc0:W_], in_=spacc[:, nc0:W_])),
                             reads=[spacc_b], writes=[spaccb_b[nsa]])
                    P.op("act", (lambda e: e.activation(out=wb[ss_][0:nk, 0:N], in_=psum[zb][0:nk, 0:N], func=AF.Exp)),
                         reads=[psum_b[zb]], writes=[wb_b[ss_]])

                def stage_C(p):
                    s = p["c"] % 2
                    n = p["n"]
                    nk = p["nk"]
                    t0, W_ = TILES[p["tj"]]
                    c0 = p["c0"]
                    N = p["N"]
                    ss_ = n % NS
                    ob = p["ob"]
                    vh = vA[s] if p["head"] == 0 else vB[s]
                    P.op("pe", (lambda e: e.matmul(psum[ob][:, c0:c0 + N], lhsT=vh[0:nk, p["i"], :], rhs=wb[ss_][0:nk, 0:N],
                                                   start=p["gfirst"], stop=p["glast"], skip_group_check=True)),
                         reads=[wb_b[ss_], v_b[s][p["vg"]]], writes=[psum_b[ob]] if (p["gfirst"] or p["glast"]) else (),
                         track=p["glast"])
                    if p["glast"]:
                        c = p["c"]
                        tj = p["tj"]
                        h0 = 64 * p["head"]
                        P.op("dve", (lambda e: e.tensor_copy(out=big2[h0:h0 + 64, c, t0:t0 + W_],
                                                             in_=psum[ob][h0:h0 + 64, 0:W_])),
                             reads=[psum_b[ob]], writes=[big2_b[c][tj]])

                run(qkv_proj(0))
                NCH = dbg.get('max_chunks', KC)
                for c in range(NCH):
                    nxt = qkv_proj(c + 1) if c + 1 < NCH else iter(())
                    lst = pairs_for_chunk(c)
                    for p in lst:
                        p["n"] = state["n"]
                        state["n"] += 1
                    sa_h = [0, 0]
                    for idx, p in enumerate(lst):
                        p["ob"] = p["head"]
                        hd = p["head"]
                        p["sa"] = sa_h[hd]
                        if not p["last"]:
                            sa_h[hd] ^= 1
                            p["nsa"] = sa_h[hd]
                    npairs = len(lst)
                    for step in range(npairs + 3):
                        if step < npairs:
                            stage_A(lst[step])
                        if 1 <= step <= npairs:
                            stage_A2(lst[step - 1])
                        if 2 <= step <= npairs + 1:
                            stage_B(lst[step - 2])
                        if step >= 3:
                            stage_C(lst[step - 3])
                        next(nxt, None)
                    run(nxt)
                for cb in range(KC):
                    run(proj("attn_w_out", j, 0, cb * 128, big2, big2_b, [0, 1, 2, 3, 4, 5], resid_evac(cb)))
                P.fence()

        def phase_output():
            with ExitStack() as ph:
                phase_norm(8, final=True, outer=ph)
                ost = [ph.enter_context(nc.sbuf_tensor(un(f"ost{i}"), [128, D], F32)) for i in range(NOS)]
                ost_b = [Buf() for _ in range(NOS)]
                bank_rr = 0
                for b in range(16):
                    s = b % NOS
                    ti = 1 + b // 4
                    col = NMETA + b * 128
                    for half in range(2):
                        bi = bank_rr % 8
                        bank_rr += 1
                        for q in range(4):
                            k = half * 4 + q
                            P.op("pe", (lambda e, k=k, q=q, bi=bi, col=col: e.transpose(
                                psum[bi][:, q * 128:(q + 1) * 128], hT[:, k, col:col + 128], identf[:, :])),
                                reads=[hT_b[kk][ti] for kk in range(half * 4, half * 4 + 4)] if q == 0 else (),
                                writes=[psum_b[bi]] if q == 0 else (), track=(q == 3))
                        eng = pick_evac()
                        if eng == "act":
                            fn = (lambda e, s=s, half=half, bi=bi: e.activation(
                                out=ost[s][:, half * 512:(half + 1) * 512], in_=psum[bi][:, 0:512], func=AF.Copy))
                        else:
                            fn = (lambda e, s=s, half=half, bi=bi: e.tensor_copy(
                                out=ost[s][:, half * 512:(half + 1) * 512], in_=psum[bi][:, 0:512]))
                        P.op(eng, fn, reads=[psum_b[bi]], writes=[ost_b[s]])
                    P.dma("sp", (lambda e, b=b, s=s: e.dma_start(out=out_d[b * 128:(b + 1) * 128, :], in_=ost[s][:])),
                          f"outst{s}", reads=[ost_b[s]])
                P.wait_all("sp")

        fuse_in = n_layers > 0 and not dbg.get('skip_conv')
        if not fuse_in:
            phase_input()
        for l in range(n_layers):
            j = l // 2
            if l % 2 == 0:
                if not dbg.get('skip_conv'):
                    phase_conv(j, with_input=(l == 0))
            else:
                phase_attn(j)
            if not dbg.get('skip_mlp'):
                phase_mlp(l)
        phase_output()

        with nc.Block() as block:
            @block.tensor
            def _(e):
                P.replay("pe", e, sems)

            @block.scalar
            def _(e):
                P.replay("act", e, sems)

            @block.vector
            def _(e):
                P.replay("dve", e, sems)

            @block.gpsimd
            def _(e):
                P.replay("pool", e, sems)

            @block.sync
            def _(e):
                P.replay("sp", e, sems)
    return nc, P


def _consts():
    bf = ml_dtypes.bfloat16
    ident = np.eye(128, dtype=np.float32)
    j = np.arange(128)[:, None]
    s = np.arange(128)[None, :]
    negtri = np.where(j >= s, -1.0, 0.0).astype(np.float32)
    negones = -np.ones((128, 128), np.float32)
    ones = np.ones((128, 128), np.float32)
    m0 = np.where(s <= j, NEG, 0.0).astype(np.float32)
    m1 = np.where(s <= j - 16, NEG, 0.0).astype(np.float32)
    cb = np.concatenate([ident, negtri, negones, ones, m0, m1], axis=1).astype(bf)
    return ident, cb


def _layout_vec(v):
    v = np.asarray(v, np.float32).reshape(-1, KC, 128)
    return np.ascontiguousarray(v.transpose(2, 0, 1).reshape(128, -1))


_CACHE = {}


def kernel(x, meta_tokens, conv_norm, conv_w_in, conv_w, conv_w_out, attn_norm, attn_w_qkv, attn_w_out,
           mlp_norm, mlp_w1, mlp_w2, final_norm, _n_layers=DEPTH, _dbg=None):
    x = np.asarray(x, np.float32)
    B = x.shape[0]
    key = (_n_layers, repr(_dbg))
    if key not in _CACHE:
        _CACHE[key] = build_nc(_n_layers, _dbg)[0]
    nc = _CACHE[key]
    ident, cb = _consts()
    gains = _layout_vec(np.concatenate([np.asarray(conv_norm), np.asarray(attn_norm), np.asarray(mlp_norm),
                                        np.asarray(final_norm)[None]], axis=0))
    cw = _layout_vec(np.asarray(conv_w).reshape(6, D))
    common = {
        "meta": np.ascontiguousarray(np.asarray(meta_tokens, np.float32)),
        "gains": gains, "cw": cw, "identf": ident, "cbf": cb,
        "conv_w_in": np.ascontiguousarray(np.asarray(conv_w_in, np.float32)),
        "conv_w_out": np.ascontiguousarray(np.asarray(conv_w_out, np.float32)),
        "attn_w_qkv": np.ascontiguousarray(np.asarray(attn_w_qkv, np.float32)),
        "attn_w_out": np.ascontiguousarray(np.asarray(attn_w_out, np.float32)),
        "mlp_w1": np.ascontiguousarray(np.asarray(mlp_w1, np.float32)),
        "mlp_w2": np.ascontiguousarray(np.asarray(mlp_w2, np.float32)),
    }
    in_maps = [dict(common, x=np.ascontiguousarray(x[b])) for b in range(B)]
    res = run_bass_kernel_spmd(nc, in_maps, core_ids=list(range(B)))
    return np.stack([np.asarray(r["out"], np.float32) for r in res.results], axis=0)
```

```python
from contextlib import ExitStack

import numpy as np
import ml_dtypes

import concourse.bass as bass
import concourse.mybir as mybir
from concourse.bass_utils import run_bass_kernel_spmd

F32 = mybir.dt.float32
BF16 = mybir.dt.bfloat16
AF = mybir.ActivationFunctionType
ALU = mybir.AluOpType

D = 1024
KC = 8
SEQ = 2048
NMETA = 16
T = SEQ + NMETA
DEPTH = 4
DFF = 4096
EPS = 1e-6
NEG = -30000.0
TILES = [(0, NMETA)] + [(NMETA + 512 * j, 512) for j in range(4)]
KBLK = [(128 * i, 128) for i in range(16)] + [(2048, NMETA)]
MLP_G = 4
N_WSLOT = 8
W_PREFETCH = 4


class Buf:
    __slots__ = ("w", "r")

    def __init__(self):
        self.w = None
        self.r = {}


class Plan:
    ENGS = ("pe", "act", "dve", "pool", "sp")

    def __init__(self):
        self.ops = {e: [] for e in self.ENGS}
        self.cnt = {e: 0 for e in self.ENGS}
        self.dma_cnt = {}
        self.fence_deps = {}

    def _deps(self, reads, writes, extra):
        deps = dict(self.fence_deps)

        def add(t):
            if t is None:
                return
            s, v = t
            if deps.get(s, 0) < v:
                deps[s] = v
        for t in extra:
            add(t)
        for b in reads:
            add(b.w)
        for b in writes:
            add(b.w)
            for s, v in b.r.items():
                add((s, v))
        return deps

    @staticmethod
    def _note(t, reads, writes):
        s, v = t
        for b in reads:
            if b.r.get(s, 0) < v:
                b.r[s] = v
        for b in writes:
            b.w = t
            b.r = {}

    def op(self, eng, fn, reads=(), writes=(), track=True, extra=()):
        deps = self._deps(reads, writes, extra)
        if deps.get(eng, 0) > self.cnt[eng]:
            deps[eng] = self.cnt[eng]
        if track:
            self.cnt[eng] += 1
            t = (eng, self.cnt[eng])
        else:
            t = (eng, self.cnt[eng] + 1)
        self.ops[eng].append((fn, deps, eng if track else None, 1))
        self._note(t, reads, writes)
        return t

    def dma(self, queue, fn, sem, reads=(), writes=(), extra=()):
        deps = self._deps(reads, writes, extra)
        self.dma_cnt[sem] = self.dma_cnt.get(sem, 0) + 16
        t = (sem, self.dma_cnt[sem])
        self.ops[queue].append((fn, deps, sem, 16))
        self._note(t, reads, writes)
        return t

    def fence(self):
        f = dict(self.fence_deps)
        for e in ("pe", "act", "dve", "pool"):
            f[e] = self.cnt[e]
        for s, v in self.dma_cnt.items():
            f[s] = v
        self.fence_deps = f

    def wait_all(self, eng):
        self.fence()
        self.ops[eng].append((None, dict(self.fence_deps), None, 0))

    def replay(self, eng_name, eng, sems):
        seen = {}
        for fn, deps, inc_sem, inc in self.ops[eng_name]:
            for s, v in deps.items():
                if v > 0 and seen.get(s, 0) < v:
                    eng.wait_ge(sems[s], v)
                    seen[s] = v
            if fn is None:
                continue
            ins = fn(eng)
            if inc_sem is not None:
                ins.then_inc(sems[inc_sem], inc)


def build_nc(n_layers=DEPTH, dbg=None):
    dbg = dbg or {}
    nc = bass.Bass("TRN2", target_bir_lowering=False)
    P = Plan()

    x_d = nc.dram_tensor("x", [SEQ, D], F32, kind="ExternalInput").ap()
    meta_d = nc.dram_tensor("meta", [NMETA, D], F32, kind="ExternalInput").ap()
    gains_d = nc.dram_tensor("gains", [128, 9 * KC], F32, kind="ExternalInput").ap()
    cw_d = nc.dram_tensor("cw", [128, 2 * 3 * KC], F32, kind="ExternalInput").ap()
    identf_d = nc.dram_tensor("identf", [128, 128], F32, kind="ExternalInput").ap()
    cb_d = nc.dram_tensor("cbf", [128, 6 * 128], BF16, kind="ExternalInput").ap()
    conv_w_in = nc.dram_tensor("conv_w_in", [2, D, 3 * D], F32, kind="ExternalInput").ap()
    conv_w_out = nc.dram_tensor("conv_w_out", [2, D, D], F32, kind="ExternalInput").ap()
    attn_w_qkv = nc.dram_tensor("attn_w_qkv", [2, D, 3 * D], F32, kind="ExternalInput").ap()
    attn_w_out = nc.dram_tensor("attn_w_out", [2, D, D], F32, kind="ExternalInput").ap()
    mlp_w1 = nc.dram_tensor("mlp_w1", [DEPTH, D, DFF], F32, kind="ExternalInput").ap()
    mlp_w2 = nc.dram_tensor("mlp_w2", [DEPTH, DFF, D], F32, kind="ExternalInput").ap()
    out_d = nc.dram_tensor("out", [SEQ, D], F32, kind="ExternalOutput").ap()

    uid = {"n": 0}

    def un(name):
        uid["n"] += 1
        return f"{name}_{uid['n']}"

    es = ExitStack()
    with es:
        def sb(name, shape, dt):
            return es.enter_context(nc.sbuf_tensor(name, shape, dt))

        hT = sb("hT", [128, KC, T], F32)
        hn = sb("hn", [128, KC, T], BF16)
        big2 = sb("big2", [128, KC, T], BF16)
        wring = [sb(f"wr{i}", [128, KC, 128], BF16) for i in range(N_WSLOT)]
        identf = sb("identf_s", [128, 128], F32)
        cbf = sb("cbf_s", [128, 6 * 128], BF16)
        gains = sb("gains_s", [128, 9 * KC], F32)
        cw = sb("cw_s", [128, 2 * 3 * KC], F32)
        identb = cbf[:, 0:128]
        negtri = cbf[:, 128:256]
        negones = cbf[:, 256:384]
        onesb = cbf[:, 384:512]
        m0 = cbf[:, 512:640]
        m1 = cbf[:, 640:768]

        psum = [es.enter_context(nc.psum_tensor(f"ps{i}", [128, 512], F32)) for i in range(8)]
        psum_b = [Buf() for _ in range(8)]

        NXS, NOS = 6, 4
        sem_names = ["pe", "act", "dve", "pool", "const", "metas"] + [f"xin{i}" for i in range(NXS)] + \
                    [f"outst{i}" for i in range(NOS)] + \
                    [f"w{i}" for i in range(N_WSLOT)]
        sems = {n: es.enter_context(nc.semaphore(n)) for n in sem_names}

        hT_b = [[Buf() for _ in TILES] for _ in range(KC)]
        hn_b = [[Buf() for _ in TILES] for _ in range(KC)]
        big2_b = [[Buf() for _ in TILES] for _ in range(KC)]
        wring_b = [Buf() for _ in range(N_WSLOT)]
        wstate = {"next": 0}
        rr = {"evac": 0}

        def tsl(ti):
            t0, w = TILES[ti]
            return slice(t0, t0 + w)

        P.dma("sp", lambda e: e.dma_start(out=identf[:], in_=identf_d[:, :]), "const")
        P.dma("sp", lambda e: e.dma_start(out=cbf[:], in_=cb_d[:, :]), "const")
        P.dma("sp", lambda e: e.dma_start(out=gains[:], in_=gains_d[:, :]), "const")
        P.dma("sp", lambda e: e.dma_start(out=cw[:], in_=cw_d[:, :]), "const")
        P.fence()

        wsrc = {"conv_w_in": conv_w_in, "conv_w_out": conv_w_out, "attn_w_qkv": attn_w_qkv,
                "attn_w_out": attn_w_out, "mlp_w1": mlp_w1, "mlp_w2": mlp_w2}
        wseq = []
        for l_ in range(n_layers):
            j_ = l_ // 2
            if l_ % 2 == 0:
                if not dbg.get('skip_conv'):
                    for c_ in range(KC):
                        for sec in range(3):
                            wseq.append(("conv_w_in", j_, 0, sec * D + c_ * 128, KC))
                    for cb_ in range(KC):
                        wseq.append(("conv_w_out", j_, 0, cb_ * 128, KC))
            else:
                for c_ in range(dbg.get('max_chunks', KC)):
                    for sec in range(3):
                        wseq.append(("attn_w_qkv", j_, 0, sec * D + c_ * 128, KC))
                for cb_ in range(KC):
                    wseq.append(("attn_w_out", j_, 0, cb_ * 128, KC))
            if not dbg.get('skip_mlp'):
                for g_ in range(DFF // (128 * MLP_G)):
                    for fc_ in range(MLP_G):
                        wseq.append(("mlp_w1", l_, 0, (g_ * MLP_G + fc_) * 128, KC))
                    for cb_ in range(KC):
                        wseq.append(("mlp_w2", l_, g_ * MLP_G * 128, cb_ * 128, MLP_G))
        wstate["use"] = 0
        wstate["issued"] = 0

        def issue_w(m):
            key, li, r0, c0, nk = wseq[m]
            i = m % N_WSLOT
            slot = wring[i]
            src = wsrc[key][li][r0:r0 + nk * 128, c0:c0 + 128].rearrange("(k p) c -> p k c", p=128)
            P.dma("pool", lambda e: e.dma_start(out=slot[:, 0:nk, :], in_=src), f"w{i}", writes=[wring_b[i]])

        def load_w(key, li, r0, c0, nk=KC):
            n = wstate["use"]
            assert wseq[n] == (key, li, r0, c0, nk), (n, wseq[n], (key, li, r0, c0, nk))
            wstate["use"] = n + 1
            while wstate["issued"] < min(len(wseq), n + 1 + W_PREFETCH):
                issue_w(wstate["issued"])
                wstate["issued"] += 1
            return n % N_WSLOT

        bankrr = {}

        def next_bank(banks):
            key = tuple(banks)
            i = bankrr.get(key, 0)
            bankrr[key] = i + 1
            return banks[i % len(banks)]

        def pick_evac():
            rr["evac"] ^= 1
            return "act" if rr["evac"] else "dve"

        def mm_group(bank_i, out_ap_fn, pairs, reads, first=True, last=True, track_last=True):
            n = len(pairs)
            t = None
            for idx, (l, r) in enumerate(pairs):
                st = first and idx == 0
                sp_ = last and idx == n - 1
                tr = track_last and idx == n - 1
                t = P.op("pe",
                         (lambda e, l=l, r=r, st=st, sp_=sp_: e.matmul(out_ap_fn(), lhsT=l, rhs=r, start=st, stop=sp_)),
                         reads=reads if idx == 0 else (), writes=[psum_b[bank_i]] if idx == 0 else (),
                         track=tr)
            return t

        def proj(wkey, li, r0, c0, in_buf, in_b, banks, evac, tiles=range(len(TILES)), fine=False):
            wi = load_w(wkey, li, r0, c0)
            for ti in tiles:
                bi = next_bank(banks)
                t0, w = TILES[ti]
                rd = [wring_b[wi]] + [in_b[k][ti] for k in range(KC)]
                for k in range(KC):
                    edge = (k == 0 or k == KC - 1)
                    P.op("pe", (lambda e, bi=bi, w=w, k=k, t0=t0: e.matmul(
                        psum[bi][:, 0:w], lhsT=wring[wi][:, k, :], rhs=in_buf[:, k, t0:t0 + w],
                        start=(k == 0), stop=(k == KC - 1))),
                        reads=rd if edge else (), writes=[psum_b[bi]] if edge else (), track=(k == KC - 1))
                    if fine and k % 2 == 1 and k < KC - 1:
                        yield
                evac(ti, bi)
                yield

        def run(gen):
            for _ in gen:
                pass

        def phase_input(outer=None):
            with ExitStack() as own:
                ph = own if outer is None else outer
                nxs = NXS if outer is None else 3
                xs = [ph.enter_context(nc.sbuf_tensor(un(f"xs{i}"), [128, D], F32)) for i in range(nxs)]
                ms = ph.enter_context(nc.sbuf_tensor(un("ms"), [NMETA, D], F32))
                xs_b = [Buf() for _ in range(NXS)]
                ms_b = Buf()
                P.dma("sp", lambda e: e.dma_start(out=ms[:], in_=meta_d[:, :]), "metas", writes=[ms_b])
                bi = 0
                for k in range(KC):
                    P.op("pe", (lambda e, k=k: e.transpose(psum[0][:, k * NMETA:(k + 1) * NMETA],
                                                           ms[0:NMETA, k * 128:(k + 1) * 128],
                                                           identf[0:NMETA, 0:NMETA])),
                         reads=[ms_b] if k == 0 else (), writes=[psum_b[0]] if k == 0 else (), track=(k == KC - 1))
                P.op("dve", lambda e: e.tensor_copy(out=hT[:, :, 0:NMETA],
                                                    in_=psum[0][:, 0:KC * NMETA].rearrange("p (k t) -> p k t", t=NMETA)),
                     reads=[psum_b[0]], writes=[hT_b[k][0] for k in range(KC)])
                bank_rr = 1
                for b in range(16):
                    s = b % nxs
                    P.dma("sp", (lambda e, b=b, s=s: e.dma_start(out=xs[s][:], in_=x_d[b * 128:(b + 1) * 128, :])),
                          f"xin{s}", writes=[xs_b[s]])
                    ti = 1 + b // 4
                    col = NMETA + b * 128
                    for half in range(2):
                        bi = 1 + (bank_rr % 7)
                        bank_rr += 1
                        for q in range(4):
                            k = half * 4 + q
                            P.op("pe", (lambda e, k=k, q=q, s=s, bi=bi: e.transpose(
                                psum[bi][:, q * 128:(q + 1) * 128], xs[s][:, k * 128:(k + 1) * 128], identf[:, :])),
                                reads=[xs_b[s]] if q == 0 else (), writes=[psum_b[bi]] if q == 0 else (),
                                track=(q == 3))
                        eng = pick_evac()
                        src = (lambda bi=bi: psum[bi][:, 0:512].rearrange("p (k t) -> p k t", t=128))
                        dst = (lambda half=half, col=col: hT[:, half * 4:half * 4 + 4, col:col + 128])
                        if eng == "act":
                            fn = (lambda e, src=src, dst=dst: e.activation(out=dst(), in_=src(), func=AF.Copy))
                        else:
                            fn = (lambda e, src=src, dst=dst: e.tensor_copy(out=dst(), in_=src()))
                        P.op(eng, fn, reads=[psum_b[bi]], writes=[hT_b[half * 4 + q][ti] for q in range(4)])
                if outer is None:
                    P.fence()

        def phase_norm(gidx, final=False, outer=None, alias_big2=False):
            own = ExitStack() if (outer is None and not alias_big2) else None
            ph = own if own is not None else outer
            if alias_big2:
                hsq_ap = [(lambda w: big2[:, :, 0:w]), (lambda w: big2[:, :, 512:512 + w])]
                hsq_tr = [[big2_b[k][ti] for k in range(KC) for ti in (0, 1)],
                          [big2_b[k][ti] for k in range(KC) for ti in (1, 2)]]
            else:
                hsq = [ph.enter_context(nc.sbuf_tensor(un(f"hsq{i}"), [128, KC, 512], BF16)) for i in range(2)]
                hsq_ap = [(lambda w, i=i: hsq[i][:, :, 0:w]) for i in range(2)]
                hsq_tr = [[Buf()], [Buf()]]
            def sq(ti):
                t0, w = TILES[ti]
                s = ti % 2
                P.op("act", (lambda e, s=s, t0=t0, w=w: e.activation(
                    out=hsq_ap[s](w), in_=hT[:, :, t0:t0 + w], func=AF.Square)),
                    reads=[hT_b[k][ti] for k in range(KC)], writes=hsq_tr[s])

            sq(0)
            for ti, (t0, w) in enumerate(TILES):
                s = ti % 2
                bi = 6 + s
                if ti + 1 < len(TILES):
                    sq(ti + 1)
                for k in range(KC):
                    P.op("pe", (lambda e, s=s, bi=bi, w=w, k=k: e.matmul(
                        psum[bi][:, 0:w], lhsT=onesb, rhs=hsq_ap[s](w)[:, k, :], start=(k == 0), stop=(k == KC - 1))),
                        reads=hsq_tr[s] if k in (0, KC - 1) else (), writes=[psum_b[bi]] if k in (0, KC - 1) else (),
                        track=(k == KC - 1))
                P.op("act", (lambda e, bi=bi, w=w: e.activation(
                    out=psum[bi][:, 0:w], in_=psum[bi][:, 0:w], func=AF.Ln, scale=1.0 / D, bias=EPS)),
                    reads=[psum_b[bi]], writes=[psum_b[bi]])
                P.op("act", (lambda e, bi=bi, w=w: e.activation(
                    out=psum[bi][:, 0:w], in_=psum[bi][:, 0:w], func=AF.Exp, scale=-0.5)),
                    reads=[psum_b[bi]], writes=[psum_b[bi]])
                for k in range(KC):
                    g_ap = gains[:, gidx * KC + k:gidx * KC + k + 1]
                    if final:
                        P.op("dve", (lambda e, k=k, bi=bi, t0=t0, w=w, g_ap=g_ap: e.scalar_tensor_tensor(
                            out=hT[:, k, t0:t0 + w], in0=hT[:, k, t0:t0 + w], scalar=g_ap,
                            in1=psum[bi][:, 0:w], op0=ALU.mult, op1=ALU.mult)),
                            reads=[psum_b[bi]], writes=[hT_b[k][ti]])
                    else:
                        P.op("dve", (lambda e, k=k, bi=bi, t0=t0, w=w, g_ap=g_ap: e.scalar_tensor_tensor(
                            out=hn[:, k, t0:t0 + w], in0=hT[:, k, t0:t0 + w], scalar=g_ap,
                            in1=psum[bi][:, 0:w], op0=ALU.mult, op1=ALU.mult)),
                            reads=[psum_b[bi], hT_b[k][ti]], writes=[hn_b[k][ti]])
            if own is not None:
                P.fence()
                own.close()

        def resid_evac(cb):
            def ev(ti, bi):
                t0, w = TILES[ti]
                P.op("dve", (lambda e: e.tensor_tensor(out=hT[:, cb, t0:t0 + w], in0=psum[bi][:, 0:w],
                                                       in1=hT[:, cb, t0:t0 + w], op=ALU.add)),
                     reads=[psum_b[bi]], writes=[hT_b[cb][ti]])
            return ev

        def phase_conv(j, with_input=False):
            with ExitStack() as ph:
                vsb = [ph.enter_context(nc.sbuf_tensor(un(f"vsb{i}"), [128, 512], F32)) for i in range(2)]
                t1 = [ph.enter_context(nc.sbuf_tensor(un(f"t1{i}"), [128, 512], F32)) for i in range(2)]
                ub = [ph.enter_context(nc.sbuf_tensor(un(f"u{i}"), [128, T + 2], F32)) for i in range(2)]
                vsb_b = [Buf(), Buf()]
                t1_b = [Buf(), Buf()]
                u_b = [[Buf() for _ in TILES] for _ in range(2)]
                uh_b = [Buf(), Buf()]
                for i in range(2):
                    P.op("pool", (lambda e, i=i: e.memset(ub[i][:, 0:2], 0.0)), writes=[uh_b[i]])
                if with_input:
                    phase_input(outer=ph)
                phase_norm(0 + j, outer=ph)
                cnt = {"n": 0}
                for c in range(KC):
                    us = c % 2
                    held = {}

                    def ev_gb(ti, bi):
                        held[("gb", ti)] = bi

                    def ev_gc(ti, bi):
                        held[("gc", ti)] = bi

                    def ev_val(ti, place, c=c, us=us):
                        t0, w = TILES[ti]
                        s = cnt["n"] % 2
                        cnt["n"] += 1
                        bi, co = place
                        bgb, cgb = held[("gb", ti)]
                        bgc, cgc = held[("gc", ti)]
                        P.op("act", (lambda e: e.activation(out=vsb[s][:, 0:w], in_=psum[bi][:, co:co + w], func=AF.Copy)),
                             reads=[psum_b[bi]], writes=[vsb_b[s]])
                        P.op("dve", (lambda e: e.tensor_tensor(out=ub[us][:, 2 + t0:2 + t0 + w], in0=psum[bgc][:, cgc:cgc + w],
                                                               in1=vsb[s][:, 0:w], op=ALU.mult)),
                             reads=[psum_b[bgc], vsb_b[s]], writes=[u_b[us][ti]])
                        prev = [u_b[us][ti - 1]] if ti > 0 else [uh_b[us]]
                        base = (j * 3) * KC + c
                        w0 = cw[:, base:base + 1]
                        w1 = cw[:, base + KC:base + KC + 1]
                        w2 = cw[:, base + 2 * KC:base + 2 * KC + 1]
                        P.op("dve", (lambda e: e.tensor_scalar(out=t1[s][:, 0:w], in0=ub[us][:, t0:t0 + w],
                                                               scalar1=w0, scalar2=None, op0=ALU.mult)),
                             reads=[u_b[us][ti]] + prev, writes=[t1_b[s]])
                        P.op("dve", (lambda e: e.scalar_tensor_tensor(out=t1[s][:, 0:w], in0=ub[us][:, 1 + t0:1 + t0 + w],
                                                                      scalar=w1, in1=t1[s][:, 0:w],
                                                                      op0=ALU.mult, op1=ALU.add)),
                             reads=[u_b[us][ti]] + prev, writes=[t1_b[s]])
                        P.op("dve", (lambda e: e.scalar_tensor_tensor(out=t1[s][:, 0:w], in0=ub[us][:, 2 + t0:2 + t0 + w],
                                                                      scalar=w2, in1=t1[s][:, 0:w],
                                                                      op0=ALU.mult, op1=ALU.add)),
                             reads=[u_b[us][ti]], writes=[t1_b[s]])
                        P.op("dve", (lambda e: e.tensor_tensor(out=big2[:, c, t0:t0 + w], in0=psum[bgb][:, cgb:cgb + w],
                                                               in1=t1[s][:, 0:w], op=ALU.mult)),
                             reads=[psum_b[bgb], t1_b[s]], writes=[big2_b[c][ti]])

                    wi_gb = load_w("conv_w_in", j, 0, c * 128)
                    wi_gc = load_w("conv_w_in", j, 0, D + c * 128)
                    wi_v = load_w("conv_w_in", j, 0, 2 * D + c * 128)
                    for ti, (t0, w) in enumerate(TILES):
                        if ti == 0:
                            b0 = next_bank([6, 7])
                            places = [(b0, 0), (b0, 16), (b0, 32)]
                        else:
                            b0 = next_bank([0, 3])
                            places = [(b0, 0), (b0 + 1, 0), (b0 + 2, 0)]
                        for n_, (wi, ev) in enumerate(((wi_gb, ev_gb), (wi_gc, ev_gc), (wi_v, ev_val))):
                            bi, co = places[n_]
                            pairs = [(wring[wi][:, k, :], hn[:, k, t0:t0 + w]) for k in range(KC)]
                            mm_group(bi, (lambda bi=bi, w=w, co=co: psum[bi][:, co:co + w]), pairs,
                                     reads=[wring_b[wi]] + [hn_b[k][ti] for k in range(KC)])
                            ev(ti, (bi, co))
                for cb in range(KC):
                    run(proj("conv_w_out", j, 0, cb * 128, big2, big2_b, [0, 1, 2, 3, 4, 5], resid_evac(cb)))
                P.fence()

        def phase_mlp(l):
            with ExitStack() as ph:
                ab = [ph.enter_context(nc.sbuf_tensor(un(f"ab{i}"), [128, MLP_G, T], BF16)) for i in range(2)]
                rt = [ph.enter_context(nc.sbuf_tensor(un(f"rt{i}"), [128, 512], F32)) for i in range(2)]
                ab_b = [[[Buf() for _ in TILES] for _ in range(MLP_G)] for _ in range(2)]
                rt_b = [Buf(), Buf()]
                cnt = {"n": 0}
                phase_norm(4 + l, outer=ph)
                ngroups = DFF // (128 * MLP_G)
                for g in range(ngroups):
                    a = g % 2
                    wis = [load_w("mlp_w1", l, 0, (g * MLP_G + fc) * 128) for fc in range(MLP_G)]
                    for ti, (t0, w) in enumerate(TILES):
                        for fc in range(MLP_G):
                            wi = wis[fc]
                            bi = next_bank([0, 1, 2, 3])
                            pairs = [(wring[wi][:, k, :], hn[:, k, t0:t0 + w]) for k in range(KC)]
                            mm_group(bi, (lambda bi=bi, w=w: psum[bi][:, 0:w]), pairs,
                                     reads=[wring_b[wi]] + [hn_b[k][ti] for k in range(KC)])
                            s = cnt["n"] % 2
                            cnt["n"] += 1
                            P.op("act", (lambda e, s=s, bi=bi, w=w: e.activation(out=rt[s][:, 0:w], in_=psum[bi][:, 0:w],
                                                                                func=AF.Relu)),
                                 reads=[psum_b[bi]], writes=[rt_b[s]])
                            P.op("pool", (lambda e, s=s, a=a, fc=fc, t0=t0, w=w: e.tensor_tensor(
                                out=ab[a][:, fc, t0:t0 + w], in0=rt[s][:, 0:w], in1=rt[s][:, 0:w], op=ALU.mult)),
                                reads=[rt_b[s]], writes=[ab_b[a][fc][ti]])
                    for cb in range(KC):
                        i = load_w("mlp_w2", l, g * MLP_G * 128, cb * 128, MLP_G)
                        slot = wring[i]
                        ev2 = resid_evac(cb)
                        for ti, (t0, w) in enumerate(TILES):
                            bi = next_bank([4, 5, 6, 7])
                            pairs = [(slot[:, k, :], ab[a][:, k, t0:t0 + w]) for k in range(MLP_G)]
                            mm_group(bi, (lambda bi=bi, w=w: psum[bi][:, 0:w]), pairs,
                                     reads=[wring_b[i]] + [ab_b[a][k][ti] for k in range(MLP_G)])
                            ev2(ti, bi)
                P.fence()

        def phase_attn(j):
            with ExitStack() as ph:
                def psb(name, shape, dt):
                    return ph.enter_context(nc.sbuf_tensor(un(name), shape, dt))
                qA = [psb(f"qA{i}", [128, T], BF16) for i in range(2)]
                qB = [psb(f"qB{i}", [128, T], BF16) for i in range(2)]
                kT = [psb(f"kT{i}", [128, T], BF16) for i in range(2)]
                vA = [psb(f"vA{i}", [128, 17, 128], BF16) for i in range(2)]
                vB = [psb(f"vB{i}", [128, 17, 128], BF16) for i in range(2)]
                NS = 3
                ebuf = [psb(f"eb{i}", [128, 512], F32) for i in range(2)]
                spb = [psb(f"spb{i}", [128, 512], BF16) for i in range(NS)]
                wb = [psb(f"wb{i}", [128, 512], BF16) for i in range(NS)]
                spacc_h = [psb(f"spacc{h}", [128, 512], F32) for h in range(2)]
                spaccb_h = [[psb(f"spaccb{h}_{i}", [128, 512], BF16) for i in range(2)] for h in range(2)]
                q_b = [[Buf() for _ in TILES] for _ in range(2)]
                k_b = [[Buf() for _ in TILES] for _ in range(2)]
                v_b = [[Buf() for _ in range(5)] for _ in range(2)]
                ebuf_b = [Buf(), Buf()]
                spb_b = [Buf() for _ in range(NS)]
                wb_b = [Buf() for _ in range(NS)]
                spacc_bh = [Buf(), Buf()]
                spaccb_bh = [[Buf(), Buf()], [Buf(), Buf()]]
                for i in range(2):
                    zb_ = Buf()
                    eng = "dve" if i == 0 else "pool"
                    P.op(eng, (lambda e, i=i: e.memset(qA[i][64:128, :], 0.0)), writes=[zb_])
                    P.op(eng, (lambda e, i=i: e.memset(qB[i][0:64, :], 0.0)), writes=[zb_])
                    P.op(eng, (lambda e, i=i: e.memset(vA[i][:, :, 64:128], 0.0)), writes=[zb_])
                    P.op(eng, (lambda e, i=i: e.memset(vB[i][:, :, 0:64], 0.0)), writes=[zb_])
                    for ti in range(len(TILES)):
                        q_b[i][ti].w = zb_.w
                    for gi in range(5):
                        v_b[i][gi].w = zb_.w
                phase_norm(2 + j, alias_big2=True)
                PROJ_BANKS = [0, 1] if False else [6, 7]

                def tiles_of(ks, nk):
                    return [ti for ti, (t0, w) in enumerate(TILES) if t0 < ks + nk and ks < t0 + w]

                def qkv_proj(c):
                    s = c % 2

                    def ev_q(ti, bi):
                        t0, w = TILES[ti]
                        P.op("dve", (lambda e: e.tensor_scalar(out=qA[s][0:64, t0:t0 + w], in0=psum[bi][0:64, 0:w],
                                                               scalar1=0.125, scalar2=None, op0=ALU.mult)),
                             reads=[psum_b[bi]], writes=[q_b[s][ti]])
                        P.op("dve", (lambda e: e.tensor_scalar(out=qB[s][64:128, t0:t0 + w], in0=psum[bi][64:128, 0:w],
                                                               scalar1=0.125, scalar2=None, op0=ALU.mult)),
                             reads=[psum_b[bi]], writes=[q_b[s][ti]])

                    def ev_k(ti, bi):
                        t0, w = TILES[ti]
                        P.op("dve", (lambda e: e.tensor_copy(out=kT[s][:, t0:t0 + w], in_=psum[bi][:, 0:w])),
                             reads=[psum_b[bi]], writes=[k_b[s][ti]])
                    yield from proj("attn_w_qkv", j, 0, c * 128, hn, hn_b, PROJ_BANKS, ev_q, fine=True)
                    yield from proj("attn_w_qkv", j, 0, D + c * 128, hn, hn_b, PROJ_BANKS, ev_k, fine=True)
                    wi = load_w("attn_w_qkv", j, 0, 2 * D + c * 128)
                    groups = [[4 * g + q for q in range(4)] for g in range(4)] + [[16]]
                    for gi, blks in enumerate(groups):
                        bi = next_bank(PROJ_BANKS)
                        for q, kb in enumerate(blks):
                            ks, nk = KBLK[kb]
                            tis = tiles_of(ks, nk)
                            rd = [wring_b[wi]] + [hn_b[kk][ti] for kk in range(KC) for ti in tis]
                            for k in range(KC):
                                edge = (k == 0 or k == KC - 1)
                                P.op("pe", (lambda e, bi=bi, q=q, nk=nk, ks=ks, k=k: e.matmul(
                                    psum[bi][0:nk, q * 128:(q + 1) * 128], lhsT=hn[:, k, ks:ks + nk],
                                    rhs=wring[wi][:, k, :], start=(k == 0), stop=(k == KC - 1))),
                                    reads=rd if edge else (), writes=[psum_b[bi]] if edge else (),
                                    track=(k == KC - 1))
                                if k == 3:
                                    yield
                            yield
                        nb = len(blks)
                        nk = KBLK[blks[0]][1]
                        kb0 = blks[0]
                        src = (lambda bi=bi, nb=nb, nk=nk: psum[bi][0:nk, 0:nb * 128].rearrange("p (b c) -> p b c", c=128))
                        P.op("dve", (lambda e, src=src, nk=nk, kb0=kb0, nb=nb: e.tensor_copy(
                            out=vA[s][0:nk, kb0:kb0 + nb, 0:64], in_=src()[:, :, 0:64])),
                            reads=[psum_b[bi]], writes=[v_b[s][gi]])
                        P.op("dve", (lambda e, src=src, nk=nk, kb0=kb0, nb=nb: e.tensor_copy(
                            out=vB[s][0:nk, kb0:kb0 + nb, 64:128], in_=src()[:, :, 64:128])),
                            reads=[psum_b[bi]], writes=[v_b[s][gi]])
                        yield

                def pairs_for_chunk(c):
                    lst = []
                    blks_of = {}
                    for tj in range(dbg.get('max_tj', len(TILES))):
                        t0, W_ = TILES[tj]
                        if tj == 0:
                            blks = [dict(i=0, nk=NMETA, c0=0, N=NMETA, mask=m0)]
                        else:
                            blks = []
                            for i in range(4 * tj, -1, -1):
                                r = i - 4 * (tj - 1)
                                if r == 4:
                                    blks.append(dict(i=i, nk=NMETA, c0=496, N=16, mask=m0))
                                elif r >= 1:
                                    blks.append(dict(i=i, nk=128, c0=128 * r - 16, N=528 - 128 * r, mask=m0))
                                elif r == 0:
                                    blks.append(dict(i=i, nk=128, c0=0, N=512, mask=m1))
                                else:
                                    blks.append(dict(i=i, nk=128, c0=0, N=512, mask=None))
                        blks_of[tj] = blks
                    ntl = len(blks_of)
                    streams = []
                    for head, order in ((0, list(range(ntl))), (1, list(range(ntl - 1, -1, -1)))):
                        st = []
                        for tj in order:
                            blks = blks_of[tj]
                            for bn, b in enumerate(blks):
                                d = dict(b)
                                d.update(c=c, tj=tj, head=head, ks=KBLK[b["i"]][0],
                                         first=(bn == 0), last=(bn == len(blks) - 1))
                                d["gfirst"] = d["first"]
                                d["glast"] = d["last"]
                                d["ktis"] = tiles_of(d["ks"], d["nk"])
                                d["vg"] = b["i"] // 4
                                d["nc0"] = blks[bn + 1]["c0"] if bn + 1 < len(blks) else None
                                st.append(d)
                        streams.append(st)
                    for a_, b_ in zip(*streams):
                        lst.append(a_)
                        lst.append(b_)
                    return lst

                state = {"n": 0, "ob": 0}
                ZE = [2, 3, 4, 5]

                def stage_A(p):
                    s = p["c"] % 2
                    n = p["n"]
                    zb = ZE[n % 4]
                    ks, nk = p["ks"], p["nk"]
                    t0, _ = TILES[p["tj"]]
                    q0 = t0 + p["c0"]
                    N = p["N"]
                    qh = qA[s] if p["head"] == 0 else qB[s]
                    msk = p["mask"]
                    P.op("pe", (lambda e: e.matmul(psum[zb][0:nk, 0:N], lhsT=kT[s][:, ks:ks + nk], rhs=qh[:, q0:q0 + N],
                                                   start=True, stop=False, skip_group_check=True)),
                         reads=[k_b[s][ti] for ti in p["ktis"]] + [q_b[s][p["tj"]]], writes=[psum_b[zb]],
                         track=(msk is None))
                    if msk is not None:
                        nm = min(N, 128)
                        P.op("pe", (lambda e: e.matmul(psum[zb][0:nk, 0:nm], lhsT=identb[:, 0:nk], rhs=msk[:, 0:nm],
                                                       start=False, stop=False, skip_group_check=True)))
                    es_ = n % 2
                    P.op("act", (lambda e: e.activation(out=ebuf[es_][0:nk, 0:N], in_=psum[zb][0:nk, 0:N], func=AF.Exp)),
                         reads=[psum_b[zb]], writes=[ebuf_b[es_]])

                def stage_A2(p):
                    n = p["n"]
                    nk = p["nk"]
                    N = p["N"]
                    es_ = n % 2
                    ss_ = n % NS
                    P.op("act", (lambda e: e.activation(out=spb[ss_][0:nk, 0:N], in_=ebuf[es_][0:nk, 0:N], func=AF.Ln,
                                                        bias=1.0, scale=1.0)),
                         reads=[ebuf_b[es_]], writes=[spb_b[ss_]])

                def stage_B(p):
                    n = p["n"]
                    zb = ZE[n % 4]
                    nk = p["nk"]
                    t0, W_ = TILES[p["tj"]]
                    c0 = p["c0"]
                    N = p["N"]
                    ss_ = n % NS
                    P.op("pe", (lambda e: e.matmul(psum[zb][0:nk, 0:N], lhsT=negtri[0:nk, 0:nk], rhs=spb[ss_][0:nk, 0:N],
                                                   start=False, stop=p["first"], skip_group_check=True)),
                         reads=[spb_b[ss_]], writes=[psum_b[zb]], track=p["first"])
                    hd = p["head"]
                    spacc = spacc_h[hd]
                    spaccb = spaccb_h[hd]
                    spacc_b = spacc_bh[hd]
                    spaccb_b = spaccb_bh[hd]
                    if not p["first"]:
                        sa = p["sa"]
                        P.op("pe", (lambda e: e.matmul(psum[zb][0:nk, 0:N], lhsT=negones[:, 0:nk],
                                                       rhs=spaccb[sa][:, c0:c0 + N], start=False, stop=True,
                                                       skip_group_check=True)),
                             reads=[spaccb_b[sa]], writes=[psum_b[zb]])
                    if p["first"] and not p["last"]:
                        P.op("pool", (lambda e: e.memset(spacc[:, 0:W_], 0.0)), writes=[spacc_b])
                    if not p["last"]:
                        P.op("dve", (lambda e: e.tensor_tensor(out=spacc[0:nk, c0:W_], in0=spacc[0:nk, c0:W_],
                                                               in1=spb[ss_][0:nk, 0:N], op=ALU.add)),
                             reads=[spb_b[ss_]], writes=[spacc_b])
                        nsa = p["nsa"]
                        nc0 = p["nc0"]
                        P.op("dve", (lambda e: e.tensor_copy(out=spaccb[nsa][:, nc0:W_], in_=spacc[:, nc0:W_])),
                             reads=[spacc_b], writes=[spaccb_b[nsa]])
                    P.op("act", (lambda e: e.activation(out=wb[ss_][0:nk, 0:N], in_=psum[zb][0:nk, 0:N], func=AF.Exp)),
                         reads=[psum_b[zb]], writes=[wb_b[ss_]])

                def stage_C(p):
                    s = p["c"] % 2
                    n = p["n"]
                    nk = p["nk"]
                    t0, W_ = TILES[p["tj"]]
                    c0 = p["c0"]
                    N = p["N"]
                    ss_ = n % NS
                    ob = p["ob"]
                    vh = vA[s] if p["head"] == 0 else vB[s]
                    P.op("pe", (lambda e: e.matmul(psum[ob][:, c0:c0 + N], lhsT=vh[0:nk, p["i"], :], rhs=wb[ss_][0:nk, 0:N],
                                                   start=p["gfirst"], stop=p["glast"], skip_group_check=True)),
                         reads=[wb_b[ss_], v_b[s][p["vg"]]], writes=[psum_b[ob]] if (p["gfirst"] or p["glast"]) else (),
                         track=p["glast"])
                    if p["glast"]:
                        c = p["c"]
                        tj = p["tj"]
                        h0 = 64 * p["head"]
                        P.op("dve", (lambda e: e.tensor_copy(out=big2[h0:h0 + 64, c, t0:t0 + W_],
                                                             in_=psum[ob][h0:h0 + 64, 0:W_])),
                             reads=[psum_b[ob]], writes=[big2_b[c][tj]])

                run(qkv_proj(0))
                NCH = dbg.get('max_chunks', KC)
                for c in range(NCH):
                    nxt = qkv_proj(c + 1) if c + 1 < NCH else iter(())
                    lst = pairs_for_chunk(c)
                    for p in lst:
                        p["n"] = state["n"]
                        state["n"] += 1
                    sa_h = [0, 0]
                    for idx, p in enumerate(lst):
                        p["ob"] = p["head"]
                        hd = p["head"]
                        p["sa"] = sa_h[hd]
                        if not p["last"]:
                            sa_h[hd] ^= 1
                            p["nsa"] = sa_h[hd]
                    npairs = len(lst)
                    for step in range(npairs + 3):
                        if 1 <= step <= npairs:
                            stage_A2(lst[step - 1])
                        if step < npairs:
                            stage_A(lst[step])
                        if 2 <= step <= npairs + 1:
                            stage_B(lst[step - 2])
                        if step >= 3:
                            stage_C(lst[step - 3])
                        next(nxt, None)
                    run(nxt)
                for cb in range(KC):
                    run(proj("attn_w_out", j, 0, cb * 128, big2, big2_b, [0, 1, 2, 3, 4, 5], resid_evac(cb)))
                P.fence()

        def phase_output():
            with ExitStack() as ph:
                phase_norm(8, final=True, outer=ph)
                ost = [ph.enter_context(nc.sbuf_tensor(un(f"ost{i}"), [128, D], F32)) for i in range(NOS)]
                ost_b = [Buf() for _ in range(NOS)]
                bank_rr = 0
                for b in range(16):
                    s = b % NOS
                    ti = 1 + b // 4
                    col = NMETA + b * 128
                    for half in range(2):
                        bi = bank_rr % 8
                        bank_rr += 1
                        for q in range(4):
                            k = half * 4 + q
                            P.op("pe", (lambda e, k=k, q=q, bi=bi, col=col: e.transpose(
                                psum[bi][:, q * 128:(q + 1) * 128], hT[:, k, col:col + 128], identf[:, :])),
                                reads=[hT_b[kk][ti] for kk in range(half * 4, half * 4 + 4)] if q == 0 else (),
                                writes=[psum_b[bi]] if q == 0 else (), track=(q == 3))
                        eng = pick_evac()
                        if eng == "act":
                            fn = (lambda e, s=s, half=half, bi=bi: e.activation(
                                out=ost[s][:, half * 512:(half + 1) * 512], in_=psum[bi][:, 0:512], func=AF.Copy))
                        else:
                            fn = (lambda e, s=s, half=half, bi=bi: e.tensor_copy(
                                out=ost[s][:, half * 512:(half + 1) * 512], in_=psum[bi][:, 0:512]))
                        P.op(eng, fn, reads=[psum_b[bi]], writes=[ost_b[s]])
                    P.dma("sp", (lambda e, b=b, s=s: e.dma_start(out=out_d[b * 128:(b + 1) * 128, :], in_=ost[s][:])),
                          f"outst{s}", reads=[ost_b[s]])
                P.wait_all("sp")

        fuse_in = n_layers > 0 and not dbg.get('skip_conv')
        if not fuse_in:
            phase_input()
        for l in range(n_layers):
            j = l // 2
            if l % 2 == 0:
                if not dbg.get('skip_conv'):
                    phase_conv(j, with_input=(l == 0))
            else:
                phase_attn(j)
            if not dbg.get('skip_mlp'):
                phase_mlp(l)
        phase_output()

        with nc.Block() as block:
            @block.tensor
            def _(e):
                P.replay("pe", e, sems)

            @block.scalar
            def _(e):
                P.replay("act", e, sems)

            @block.vector
            def _(e):
                P.replay("dve", e, sems)

            @block.gpsimd
            def _(e):
                P.replay("pool", e, sems)

            @block.sync
            def _(e):
                P.replay("sp", e, sems)
    return nc, P


def _consts():
    bf = ml_dtypes.bfloat16
    ident = np.eye(128, dtype=np.float32)
    j = np.arange(128)[:, None]
    s = np.arange(128)[None, :]
    negtri = np.where(j >= s, -1.0, 0.0).astype(np.float32)
    negones = -np.ones((128, 128), np.float32)
    ones = np.ones((128, 128), np.float32)
    m0 = np.where(s <= j, NEG, 0.0).astype(np.float32)
    m1 = np.where(s <= j - 16, NEG, 0.0).astype(np.float32)
    cb = np.concatenate([ident, negtri, negones, ones, m0, m1], axis=1).astype(bf)
    return ident, cb


def _layout_vec(v):
    v = np.asarray(v, np.float32).reshape(-1, KC, 128)
    return np.ascontiguousarray(v.transpose(2, 0, 1).reshape(128, -1))


_CACHE = {}


def kernel(x, meta_tokens, conv_norm, conv_w_in, conv_w, conv_w_out, attn_norm, attn_w_qkv, attn_w_out,
           mlp_norm, mlp_w1, mlp_w2, final_norm, _n_layers=DEPTH, _dbg=None):
    x = np.asarray(x, np.float32)
    B = x.shape[0]
    key = (_n_layers, repr(_dbg))
    if key not in _CACHE:
        _CACHE[key] = build_nc(_n_layers, _dbg)[0]
    nc = _CACHE[key]
    ident, cb = _consts()
    gains = _layout_vec(np.concatenate([np.asarray(conv_norm), np.asarray(attn_norm), np.asarray(mlp_norm),
                                        np.asarray(final_norm)[None]], axis=0))
    cw = _layout_vec(np.asarray(conv_w).reshape(6, D))
    common = {
        "meta": np.ascontiguousarray(np.asarray(meta_tokens, np.float32)),
        "gains": gains, "cw": cw, "identf": ident, "cbf": cb,
        "conv_w_in": np.ascontiguousarray(np.asarray(conv_w_in, np.float32)),
        "conv_w_out": np.ascontiguousarray(np.asarray(conv_w_out, np.float32)),
        "attn_w_qkv": np.ascontiguousarray(np.asarray(attn_w_qkv, np.float32)),
        "attn_w_out": np.ascontiguousarray(np.asarray(attn_w_out, np.float32)),
        "mlp_w1": np.ascontiguousarray(np.asarray(mlp_w1, np.float32)),
        "mlp_w2": np.ascontiguousarray(np.asarray(mlp_w2, np.float32)),
    }
    in_maps = [dict(common, x=np.ascontiguousarray(x[b])) for b in range(B)]
    res = run_bass_kernel_spmd(nc, in_maps, core_ids=list(range(B)))
    return np.stack([np.asarray(r["out"], np.float32) for r in res.results], axis=0)
```
